# Optimizing a Trainium2 kernel written in Bass

```python
import math
import jax, jax.numpy as jnp
from jax import lax
import numpy as np

D_MODEL = 1024
BATCH = 32
SEQ = 256
DEPTH = 2
DEC_BATCH = 8
DEC_SEQ = 4096
PAST_LEN = 512

GRID_W = 64
N_EVEN = (DEPTH + 1) // 2
N_ODD = DEPTH // 2
EPS = 1e-6
A_WIDTH = D_MODEL // 2
A_HEADS = 4
A_DK = A_WIDTH // A_HEADS
A_DV = A_WIDTH // A_HEADS
CHUNK = 64
B_WIDTH = D_MODEL // 2
S5_GROUP = 16
S5_GROUPS = B_WIDTH // S5_GROUP
S5_P = 64
C_WIDTH = D_MODEL
CONV_W = 3
EVEN_IN = 5 * A_WIDTH + 2 * B_WIDTH
EVEN_SPLITS = (A_WIDTH, 2 * A_WIDTH, 3 * A_WIDTH, 4 * A_WIDTH, 5 * A_WIDTH, 5 * A_WIDTH + B_WIDTH)
ODD_IN = 4 * C_WIDTH

kernel_name = "hybrid_hgrn2_s5_shortconv_diffusion_step"

F32 = jnp.float32


def _rmsnorm(x, g):
    xf = x.astype(F32)
    y = xf * lax.rsqrt(jnp.mean(xf * xf, axis=-1, keepdims=True) + EPS) * g.astype(F32)
    return y.astype(x.dtype)


def _adaln(cvec, w, b):
    m = jnp.dot(jax.nn.silu(cvec.astype(F32)), w.astype(F32)) + b.astype(F32)
    return jnp.split(m, 3, axis=-1)


def _modulate(h, shift, scale):
    return (h.astype(F32) * (1.0 + scale[:, None, :]) + shift[:, None, :]).astype(h.dtype)


def _hgrn2_chunk_scan(q, k, v, logf, s0):
    bsz, L, H, _ = q.shape
    n = L // CHUNK

    def to_chunks(t):
        return t.reshape(bsz, n, CHUNK, H, t.shape[-1]).transpose(1, 0, 3, 2, 4)

    mask = jnp.tril(jnp.ones((CHUNK, CHUNK), dtype=bool))[:, :, None]

    def step(S, inp):
        qc, kc, vc, gc = inp
        b = jnp.cumsum(gc, axis=2)
        diff = b[:, :, :, None, :] - b[:, :, None, :, :]
        decay = jnp.exp(jnp.where(mask, diff, -jnp.inf))
        scores = jnp.einsum('bhtk,bhtsk,bhsk->bhts', qc, decay, kc)
        o = (jnp.einsum('bhts,bhsv->bhtv', scores, vc)
             + jnp.einsum('bhtk,bhkv->bhtv', qc * jnp.exp(b), S))
        b_last = b[:, :, -1:, :]
        S_new = (jnp.exp(b_last[:, :, 0, :])[..., None] * S
                 + jnp.einsum('bhsk,bhsv->bhkv', kc * jnp.exp(b_last - b), vc))
        return S_new, o

    S_fin, o = lax.scan(step, s0, (to_chunks(q), to_chunks(k), to_chunks(v), to_chunks(logf)))
    o = o.transpose(1, 0, 3, 2, 4).reshape(bsz, L, H, v.shape[-1])
    return o, S_fin


def _hgrn2_dir(q, f_pre, v, lb, s0):
    f = lb + (1.0 - lb) * jax.nn.sigmoid(f_pre)
    heads = lambda t: t.reshape(t.shape[0], t.shape[1], A_HEADS, -1)
    return _hgrn2_chunk_scan(heads(q), heads(1.0 - f), heads(v), heads(jnp.log(f)), s0)


def _complex_affine_combine(e1, e2):
    a1r, a1i, b1r, b1i = e1
    a2r, a2i, b2r, b2i = e2
    return (a2r * a1r - a2i * a1i,
            a2r * a1i + a2i * a1r,
            a2r * b1r - a2i * b1i + b2r,
            a2r * b1i + a2i * b1r + b2i)


def _s5_mixer(u, lam_re, lam_im, log_dt, b_re, b_im, c_re, c_im, d_skip, w_glu, b_glu, x0_re, x0_im):
    bsz, L, _ = u.shape
    uf = u.reshape(bsz, L, S5_GROUPS, S5_GROUP)
    dt = jnp.exp(log_dt)[..., None]
    mag = jnp.exp(lam_re * dt)
    lbar_re = mag * jnp.cos(lam_im * dt)
    lbar_im = mag * jnp.sin(lam_im * dt)
    den = lam_re * lam_re + lam_im * lam_im
    coef_re = ((lbar_re - 1.0) * lam_re + lbar_im * lam_im) / den
    coef_im = (lbar_im * lam_re - (lbar_re - 1.0) * lam_im) / den
    bb_re = coef_re[..., None] * b_re - coef_im[..., None] * b_im
    bb_im = coef_re[..., None] * b_im + coef_im[..., None] * b_re

    def one_dir(d, useq):
        lr, li = lbar_re[d], lbar_im[d]
        xr0, xi0 = x0_re[:, d], x0_im[:, d]
        bu_re = jnp.einsum('blgs,gps->blgp', useq, bb_re[d])
        bu_im = jnp.einsum('blgs,gps->blgp', useq, bb_im[d])
        bu_re = bu_re.at[:, 0].add(lr * xr0 - li * xi0)
        bu_im = bu_im.at[:, 0].add(lr * xi0 + li * xr0)
        a_re = jnp.broadcast_to(lr, bu_re.shape)
        a_im = jnp.broadcast_to(li, bu_im.shape)
        _, _, xr, xi = lax.associative_scan(_complex_affine_combine, (a_re, a_im, bu_re, bu_im), axis=1)
        y = jnp.einsum('blgp,gsp->blgs', xr, c_re[d]) - jnp.einsum('blgp,gsp->blgs', xi, c_im[d])
        return y, xr[:, -1], xi[:, -1]

    y_f, xr_f, xi_f = one_dir(0, uf)
    y_b, xr_b, xi_b = one_dir(1, uf[:, ::-1])
    y = y_f + y_b[:, ::-1] + d_skip.reshape(S5_GROUPS, S5_GROUP) * uf
    y = jax.nn.gelu(y.reshape(bsz, L, B_WIDTH))
    y = y * jax.nn.sigmoid(jnp.dot(y, w_glu) + b_glu)
    return y, jnp.stack([xr_f, xr_b], axis=1), jnp.stack([xi_f, xi_b], axis=1)


def _even_mixer(h, w_in, w_out, lb, hgrn_g, lam_re, lam_im, log_dt, b_re, b_im, c_re, c_im,
                d_skip, w_glu, b_glu, s_hgrn, s_re, s_im):
    bsz, L, _ = h.shape
    z = jnp.dot(h, w_in).astype(F32)
    q, f_fw, f_bw, v, g_a, u, g_b = jnp.split(z, EVEN_SPLITS, axis=-1)
    s_hgrn = s_hgrn.astype(F32)
    o_f, s_f = _hgrn2_dir(q, f_fw, v, lb, s_hgrn[:, 0])
    o_b, s_b = _hgrn2_dir(q[:, ::-1], f_bw[:, ::-1], v[:, ::-1], lb, s_hgrn[:, 1])
    o = o_f + o_b[:, ::-1]
    o = o * lax.rsqrt(jnp.mean(o * o, axis=-1, keepdims=True) + EPS) * hgrn_g.astype(F32).reshape(A_HEADS, A_DV)
    o_a = o.reshape(bsz, L, A_WIDTH) * jax.nn.silu(g_a)
    y_b, n_re, n_im = _s5_mixer(u, lam_re.astype(F32), lam_im.astype(F32), log_dt.astype(F32),
                                b_re.astype(F32), b_im.astype(F32), c_re.astype(F32), c_im.astype(F32),
                                d_skip.astype(F32), w_glu.astype(F32), b_glu.astype(F32),
                                s_re.astype(F32), s_im.astype(F32))
    o_b2 = y_b * jax.nn.silu(g_b)
    out = jnp.dot(jnp.concatenate([o_a, o_b2], axis=-1).astype(h.dtype), w_out)
    return out, jnp.stack([s_f, s_b], axis=1), n_re, n_im


def _row_conv(z, w, b, rows):
    bsz, L, ch = z.shape
    zr = z.reshape(bsz * rows, L // rows, ch)
    zp = jnp.pad(zr, ((0, 0), (1, 1), (0, 0)))
    out = w[0] * zp[:, :-2] + w[1] * zp[:, 1:-1] + w[2] * zp[:, 2:] + b
    return out.reshape(bsz, L, ch)


def _odd_mixer(h, w_in, w_out, conv_w, conv_b, rows):
    z = jnp.dot(h, w_in).astype(F32)
    bg, cg, v, g = jnp.split(z, 4, axis=-1)
    y = bg * _row_conv(cg * v, conv_w.astype(F32), conv_b.astype(F32), rows)
    return jnp.dot((y * jax.nn.silu(g)).astype(h.dtype), w_out)


def setup_inputs(seed: int = 0) -> dict:
    key = jax.random.key(seed)
    ks = jax.random.split(key, 32)
    D = D_MODEL
    nrm = lambda k, shape, s: jax.random.normal(k, shape, F32) * s
    G, P, S = S5_GROUPS, S5_P, S5_GROUP
    n_idx = jnp.arange(P, dtype=F32)
    return {
        "x_prompt": nrm(ks[0], (BATCH, SEQ, D), 1.0),
        "x_sample": nrm(ks[1], (DEC_BATCH, DEC_SEQ, D), 1.0),
        "state_hgrn": nrm(ks[2], (DEC_BATCH, N_EVEN, 2, A_HEADS, A_DK, A_DV), 0.5),
        "state_s5_re": nrm(ks[3], (DEC_BATCH, N_EVEN, 2, G, P), 0.1),
        "state_s5_im": nrm(ks[4], (DEC_BATCH, N_EVEN, 2, G, P), 0.1),
        "c": nrm(ks[5], (DEC_BATCH, D), 1.0),
        "c_ctx": nrm(ks[6], (D,), 1.0),
        "norm_g": 1.0 + nrm(ks[7], (DEPTH, D), 0.05),
        "w_mod": nrm(ks[8], (DEPTH, D, 3 * D), 0.5 * D ** -0.5),
        "b_mod": nrm(ks[9], (DEPTH, 3 * D), 0.02),
        "w_in_even": nrm(ks[10], (N_EVEN, D, EVEN_IN), D ** -0.5),
        "w_out_even": nrm(ks[11], (N_EVEN, A_WIDTH + B_WIDTH, D), (A_WIDTH + B_WIDTH) ** -0.5),
        "lb_logits": nrm(ks[12], (N_EVEN + 1, A_WIDTH), 0.5),
        "hgrn_norm_g": 1.0 + nrm(ks[13], (N_EVEN, A_WIDTH), 0.05),
        "s5_lam_re": -0.5 + nrm(ks[14], (N_EVEN, 2, G, P), 0.01),
        "s5_lam_im": jnp.pi * n_idx + nrm(ks[15], (N_EVEN, 2, G, P), 0.01),
        "s5_log_dt": jax.random.uniform(ks[16], (N_EVEN, 2, G), F32, math.log(1e-3), math.log(1e-1)),
        "s5_b_re": nrm(ks[17], (N_EVEN, 2, G, P, S), (2 * S) ** -0.5),
        "s5_b_im": nrm(ks[18], (N_EVEN, 2, G, P, S), (2 * S) ** -0.5),
        "s5_c_re": nrm(ks[19], (N_EVEN, 2, G, S, P), P ** -0.5),
        "s5_c_im": nrm(ks[20], (N_EVEN, 2, G, S, P), P ** -0.5),
        "s5_d": nrm(ks[21], (N_EVEN, B_WIDTH), 1.0),
        "w_glu": nrm(ks[22], (N_EVEN, B_WIDTH, B_WIDTH), B_WIDTH ** -0.5),
        "b_glu": nrm(ks[23], (N_EVEN, B_WIDTH), 0.02),
        "w_in_odd": nrm(ks[24], (N_ODD, D, ODD_IN), D ** -0.5),
        "w_out_odd": nrm(ks[25], (N_ODD, C_WIDTH, D), C_WIDTH ** -0.5),
        "conv_w": nrm(ks[26], (N_ODD, CONV_W, C_WIDTH), CONV_W ** -0.5),
        "conv_b": nrm(ks[27], (N_ODD, C_WIDTH), 0.02),
        "final_norm_g": 1.0 + nrm(ks[28], (D,), 0.05),
    }


def reference(x_prompt, x_sample, state_hgrn, state_s5_re, state_s5_im, c, c_ctx, norm_g, w_mod, b_mod,
              w_in_even, w_out_even, lb_logits, hgrn_norm_g, s5_lam_re, s5_lam_im, s5_log_dt,
              s5_b_re, s5_b_im, s5_c_re, s5_c_im, s5_d, w_glu, b_glu, w_in_odd, w_out_odd,
              conv_w, conv_b, final_norm_g):
    rows = x_sample.shape[1] // GRID_W
    n_ctx = x_prompt.shape[0]
    lb_all = jnp.cumsum(jax.nn.softmax(lb_logits.astype(F32), axis=0), axis=0)
    zero_hgrn = jnp.zeros((n_ctx, 2, A_HEADS, A_DK, A_DV), F32)
    zero_s5 = jnp.zeros((n_ctx, 2, S5_GROUPS, S5_P), F32)
    yp, ys = x_prompt, x_sample
    st_hgrn, st_re, st_im = [], [], []
    for l in range(DEPTH):
        sh_p, sc_p, g_p = _adaln(c_ctx[None, :], w_mod[l], b_mod[l])
        sh_s, sc_s, g_s = _adaln(c, w_mod[l], b_mod[l])
        hp = _modulate(_rmsnorm(yp, norm_g[l]), sh_p, sc_p)
        hs = _modulate(_rmsnorm(ys, norm_g[l]), sh_s, sc_s)
        j = l // 2
        if l % 2 == 0:
            ev = (w_in_even[j], w_out_even[j], lb_all[j], hgrn_norm_g[j], s5_lam_re[j], s5_lam_im[j],
                  s5_log_dt[j], s5_b_re[j], s5_b_im[j], s5_c_re[j], s5_c_im[j], s5_d[j], w_glu[j], b_glu[j])
            op, sh_new, re_new, im_new = _even_mixer(hp, *ev, zero_hgrn, zero_s5, zero_s5)
            os_, _, _, _ = _even_mixer(hs, *ev, state_hgrn[:, j], state_s5_re[:, j], state_s5_im[:, j])
            st_hgrn.append(sh_new.astype(x_prompt.dtype))
            st_re.append(re_new.astype(x_prompt.dtype))
            st_im.append(im_new.astype(x_prompt.dtype))
        else:
            op = _odd_mixer(hp, w_in_odd[j], w_out_odd[j], conv_w[j], conv_b[j], 1)
            os_ = _odd_mixer(hs, w_in_odd[j], w_out_odd[j], conv_w[j], conv_b[j], rows)
        yp = (yp.astype(F32) + g_p[:, None, :] * op.astype(F32)).astype(x_prompt.dtype)
        ys = (ys.astype(F32) + g_s[:, None, :] * os_.astype(F32)).astype(x_sample.dtype)
    y_prompt = _rmsnorm(yp, final_norm_g)
    y_sample = _rmsnorm(ys, final_norm_g)
    new_state_hgrn = jnp.stack(st_hgrn, axis=1)
    new_state_s5_re = jnp.stack(st_re, axis=1)
    new_state_s5_im = jnp.stack(st_im, axis=1)
    return (y_prompt, y_sample, new_state_hgrn, new_state_s5_re, new_state_s5_im)
```

```python
import numpy as np
from contextlib import ExitStack
import concourse.bass as bass
import concourse.mybir as mybir
from concourse.bass_utils import run_bass_kernel_spmd

F32 = mybir.dt.float32
BF16 = mybir.dt.bfloat16
AF = mybir.ActivationFunctionType
ALU = mybir.AluOpType

NBLK = 10
EPS = 1e-6
NSLOT = 5


class DSem:
    def __init__(self, sem, exclusive=True):
        self.sem = sem
        self.count = 0
        self.exclusive = exclusive


class DPool:
    def __init__(self, ds):
        self.ds = ds
        self.i = 0

    def next(self):
        d = self.ds[self.i % len(self.ds)]
        self.i += 1
        return d


class Sched:
    def __init__(self, nc, es):
        self.nc = nc
        self.es = es
        self.engs = {}
        self.sems = []
        self.dsems = []
        for n in ["pe", "act", "dve", "pool", "sp"]:
            s = self.newsem("s_" + n)
            self.engs[n] = dict(ops=[], count=0, sem=s, known={})
        self.lastw = {}
        self.readers = {}

    def newsem(self, name):
        s = self.es.enter_context(self.nc.semaphore(name))
        self.sems.append(s)
        return len(self.sems) - 1

    def dsem(self, name, exclusive=True):
        d = DSem(self.newsem(name), exclusive)
        self.dsems.append(d)
        return d

    def dpool(self, name, n):
        return DPool([self.dsem("%s%d" % (name, i)) for i in range(n)])

    def op(self, eng, fn, reads=(), writes=(), dsem=None):
        if isinstance(dsem, DPool):
            dsem = dsem.next()
        E = self.engs[eng]
        need = {}
        for k in reads:
            t = self.lastw.get(k)
            if t is not None:
                need[t[0]] = max(need.get(t[0], 0), t[1])
        for k in writes:
            t = self.lastw.get(k)
            if t is not None:
                need[t[0]] = max(need.get(t[0], 0), t[1])
            for s, v in self.readers.get(k, {}).items():
                need[s] = max(need.get(s, 0), v)
        if dsem is not None and dsem.exclusive and dsem.count > 0:
            need[dsem.sem] = max(need.get(dsem.sem, 0), dsem.count)
        waits = []
        for s, v in need.items():
            if eng == "pe" and s == E["sem"]:
                continue
            if E["known"].get(s, 0) >= v:
                continue
            E["known"][s] = v
            waits.append((s, v))
        if dsem is None:
            E["count"] += 1
            tok = (E["sem"], E["count"])
            inc = (E["sem"], 1)
        else:
            dsem.count += 16
            tok = (dsem.sem, dsem.count)
            inc = (dsem.sem, 16)
        E["ops"].append((waits, fn, inc))
        for k in reads:
            r = self.readers.setdefault(k, {})
            r[tok[0]] = max(r.get(tok[0], 0), tok[1])
        for k in writes:
            self.lastw[k] = tok
            self.readers[k] = {}
        return tok

    def emit(self):
        nc = self.nc
        sems = self.sems
        S = self

        def run(eng, name):
            for waits, fn, inc in S.engs[name]["ops"]:
                for s, v in waits:
                    eng.wait_ge(sems[s], v)
                ins = fn(eng)
                ins.then_inc(sems[inc[0]], inc[1])

        with nc.Block() as block:
            @block.tensor
            def _(eng):
                run(eng, "pe")

            @block.scalar
            def _(eng):
                run(eng, "act")

            @block.vector
            def _(eng):
                run(eng, "dve")

            @block.gpsimd
            def _(eng):
                run(eng, "pool")

            @block.sync
            def _(eng):
                run(eng, "sp")
                for d in S.dsems:
                    if d.count > 0:
                        eng.wait_ge(sems[d.sem], d.count)


def _consts():
    c = np.zeros((128, 1024), np.float32)
    c[:, 0:128] = np.eye(128, dtype=np.float32)
    c[:, 128:256] = np.eye(128, dtype=np.float32)[::-1]
    s = np.arange(128)[:, None]
    t = np.arange(128)[None, :]
    c[:, 256:384] = ((s // 64 == t // 64) & (s <= t)).astype(np.float32)
    c[:, 384] = (np.arange(128) < 64)
    c[:, 385] = (np.arange(128) >= 64)
    rm = np.ones(512, np.float32)
    rm[::64] = 0.0
    c[:, 386:898] = rm[None, :]
    return c


def build_nc(stage=9, nblk1=NBLK, nblk2=NBLK, setup=9, s5lvl=9, eblvl=9, dbg=False):
    nc = bass.Bass("TRN2", target_bir_lowering=False)
    D = 1024

    def din(name, shape):
        return nc.dram_tensor(name, list(shape), F32, kind="ExternalInput").ap()

    def dout(name, shape):
        return nc.dram_tensor(name, list(shape), F32, kind="ExternalOutput").ap()

    xs = din("xs", [5120, D])
    cvec = din("cvec", [2, D])
    st_h = din("st_h", [2, 4, 128, 128])
    st_re = din("st_re", [2, 32, 64])
    st_im = din("st_im", [2, 32, 64])
    norm_g = din("norm_g", [2, D])
    w_mod = din("w_mod", [2, D, 3 * D])
    b_mod = din("b_mod", [2, 3 * D])
    w_in_even = din("w_in_even", [D, 3584])
    w_out_even = din("w_out_even", [D, D])
    lb_logits = din("lb_logits", [2, 512])
    hgrn_norm_g = din("hgrn_norm_g", [512])
    s5_lam_re = din("s5_lam_re", [2, 32, 64])
    s5_lam_im = din("s5_lam_im", [2, 32, 64])
    s5_log_dt = din("s5_log_dt", [2, 32])
    s5_b_re = din("s5_b_re", [2, 32, 64, 16])
    s5_b_im = din("s5_b_im", [2, 32, 64, 16])
    s5_c_re = din("s5_c_re", [2, 32, 16, 64])
    s5_c_im = din("s5_c_im", [2, 32, 16, 64])
    s5_d = din("s5_d", [512])
    w_glu = din("w_glu", [512, 512])
    b_glu = din("b_glu", [512])
    w_in_odd = din("w_in_odd", [D, 4096])
    w_out_odd = din("w_out_odd", [D, D])
    conv_w = din("conv_w", [3, D])
    conv_b = din("conv_b", [D])
    final_norm_g = din("final_norm_g", [D])
    consts = din("consts", [128, 1024])

    ys = dout("ys", [5120, D])
    nh = dout("nh", [4, 2, 4, 128, 128])
    nre = dout("nre", [4, 2, 32, 64])
    nim = dout("nim", [4, 2, 32, 64])

    NUNIT = 92
    if dbg:
        dbg_bf = nc.dram_tensor('dbg_bf', [8, 128, 512], BF16, kind='ExternalOutput').ap()
        dbg_x = nc.dram_tensor('dbg_x', [128, 4, 1024], F32, kind='ExternalOutput').ap()
    wscr = nc.dram_tensor("wscr", [NUNIT, 128, 1024], BF16, kind="Internal").ap()
    obscr = nc.dram_tensor("obscr", [NBLK * 4, 128, 512], BF16, kind="Internal").ap()
    ybscr = nc.dram_tensor("ybscr", [NBLK * 4, 128, 512], BF16, kind="Internal").ap()

    U_EVOUT = 28
    U_ODIN = 44
    U_ODOUT = 76

    es = ExitStack()
    with es:
        S = Sched(nc, es)

        def sb(name, shape, dt=F32):
            return es.enter_context(nc.sbuf_tensor(name, list(shape), dt))

        CF = sb("CF", [128, 1024])
        identb = sb("identb", [128, 128], BF16)
        Jb = sb("Jb", [128, 128], BF16)
        maskf4 = sb("maskf4", [128, 512], BF16)
        ones128 = sb("ones128", [128, 128], BF16)
        ones_row = sb("ones_row", [1, 128])
        X = sb("X", [128, 4, 1024])
        XN = sb("XN", [128, 4, 1024], BF16)
        hT = sb("hT", [128, 8, 512], BF16)
        WR = [sb("WR%d" % i, [128, 8, 128], BF16) for i in range(NSLOT)]
        WG = sb("WG", [128, 4, 512], BF16)
        FG = sb("FG", [128, 1024])
        ss = sb("ss", [128, 4])
        ss8 = sb("ss8", [128, 8])
        rstd = sb("rstd", [128, 4])
        cT = sb("cT", [128, 8, 2])
        csT = sb("csT", [128, 8, 2])
        bmodT = sb("bmodT", [128, 2, 16])
        bmod_row = sb("bmod_row", [1, 2, 1024])
        normgT = sb("normgT", [128, 2, 8])
        MODT = sb("MODT", [128, 2, 16, 2])
        AT = sb("AT", [128, 2, 8, 2])
        grow = sb("grow", [1, 1024])
        lbl = sb("lbl", [128, 2, 4])
        lb = sb("lb", [128, 4])
        oml = sb("oml", [128, 4])
        noml = sb("noml", [128, 4])
        hg = sb("hg", [128, 4])
        c16 = [sb("c16_%d" % i, [128, 16]) for i in range(16)]
        Bst = [sb("Bst%d" % i, [128, 16, 16]) for i in range(2)]
        Bbar = [sb("Bbar%d" % i, [128, 16, 16]) for i in range(2)]
        Btmp = sb("Btmp", [128, 16, 16])
        M4 = [sb("M4_%d" % q, [128, 128], BF16) for q in range(4)]
        Cpads = [sb("Cpad%d" % i, [128, 128]) for i in range(2)]
        Cpb = sb("Cpb", [128, 128], BF16)
        BTt = [[sb("BTt%d%d" % (a, p), [128, 128], BF16) for p in range(2)] for a in range(4)]
        Cst = [[sb("Cst%d%d" % (a, p), [128, 128], BF16) for p in range(2)] for a in range(4)]
        BTt3 = [[sb("BTt3%d%d" % (a, p), [128, 128], BF16) for p in range(2)] for a in range(4)]
        Cst3 = [[sb("Cst3%d%d" % (a, p), [128, 64], BF16) for p in range(2)] for a in range(4)]
        Tc = sb("Tc", [128, 16, 64])
        Ts = sb("Ts", [128, 16, 64])
        Mg = sb("Mg", [128, 16, 64])
        LL1 = sb("LL1", [128, 2, 16])
        LL2 = sb("LL2", [128, 2, 16])
        Z = sb("Z", [128, 3, 16])
        zt = [sb("zt%d" % i, [128, 2, 16]) for i in range(2)]
        dT = sb("dT", [128, 4])
        Ddiag = [sb("Ddiag%d" % a, [128, 128], BF16) for a in range(4)]
        bgluT = sb("bgluT", [128, 4])
        cwT = sb("cwT", [128, 3, 8])
        cbT = sb("cbT", [128, 8])
        TT = [sb("TT%d" % i, [128, 512]) for i in range(5)]
        Qtall = sb("Qtall", [128, 4, 512], BF16)
        Qt = [Qtall[:, h, :] for h in range(4)]
        Ktall = sb("Ktall", [128, 4, 512], BF16)
        Kt = [Ktall[:, h, :] for h in range(4)]
        V = sb("V", [128, 4, 512], BF16)
        Ktok = sb("Ktok", [128, 4, 512], BF16)
        Abuf = [sb("Abuf%d" % i, [128, 512], BF16) for i in range(2)]
        Sin = [sb("Sin%d" % i, [128, 4, 128], BF16) for i in range(4)]
        S32 = sb("S32", [128, 4, 128])
        EB = sb("EB", [128, 8, 4])
        GAall = sb("GAall", [128, 4, 512], BF16)
        GA = [GAall[:, h, :] for h in range(4)]
        OAall = sb("OAall", [128, 4, 512], BF16)
        OA = [OAall[:, h, :] for h in range(4)]
        OBT = sb("OBT", [128, 2, 512], BF16)
        YBT = sb("YBT", [128, 2, 512], BF16)
        OST = [sb("OST%d" % i, [128, 512], BF16) for i in range(2)]
        osb = TT[4]
        osq = sb("osq", [128, 512], BF16)
        uTall = sb("uTall", [128, 4, 512], BF16)
        uT = [uTall[:, a, :] for a in range(4)]
        GBall = sb("GBall", [128, 4, 512], BF16)
        GBs = [GBall[:, a, :] for a in range(4)]
        OBall = sb("OBall", [128, 4, 512], BF16)
        OB = [OBall[:, a, :] for a in range(4)]
        EB2 = sb("EB2", [128, 8, 4])
        WMM = sb("WMM", [128, 4096])
        W = WMM[:, 0:2048].rearrange("p (r a j) -> p r a j", r=2, a=16)
        MM = WMM[:, 2048:4096].rearrange("p (r a j) -> p r a j", r=2, a=16)
        W1 = sb("W1", [128, 2, 16, 64])
        XP = sb("XP", [128, 2, 16, 64])
        GB = XP[:].rearrange("p r a j -> p (r a j)")[:, 0:1024]
        Ttmp = [XP[:, 0], XP[:, 1]]
        XPb = sb("XPb", [128, 2, 16, 64], BF16)
        XH = sb("XH", [128, 2, 16, 64], BF16)
        Tcb = sb("Tcb", [128, 16, 64], BF16)
        Tsb = sb("Tsb", [128, 16, 64], BF16)
        ftmp = sb("ftmp", [128, 16, 64], BF16)
        LG1 = sb("LG1", [128, 2, 16])
        LG2 = sb("LG2", [128, 2, 16])
        Zo = sb("Zo", [128, 2, 16])
        Y2b = sb("Y2b", [128, 512], BF16)
        sg2 = osq

        PS = [es.enter_context(nc.psum_tensor("PS%d" % i, [128, 512], F32)) for i in range(8)]

        d_c = S.dpool("d_c", 8)
        d_x = S.dsem("d_x")
        d_xf = S.dpool("d_xf", 2)
        d_stg = S.dsem("d_stg")
        d_stg2 = S.dsem("d_stg2")
        d_wst = S.dpool("d_wst", 4)
        d_slot = [S.dsem("d_slot%d" % i) for i in range(NSLOT)]
        d_out = S.dsem("d_out", exclusive=False)
        d_ob = S.dpool("d_ob", 3)
        d_yb = S.dpool("d_yb", 3)
        d_obl = S.dsem("d_obl")
        d_ybl = S.dsem("d_ybl")
        d_st = S.dsem("d_st")
        d_so = S.dpool("d_so", 4)
        POOLQ = [d_obl, d_ybl, d_ob, d_yb, d_out, d_so, d_st]

        def A(fn, r=(), w=()):
            return S.op("act", fn, r, w)

        def Vv(fn, r=(), w=()):
            return S.op("dve", fn, r, w)

        def Pl(fn, r=(), w=()):
            return S.op("pool", fn, r, w)

        def PE(fn, r=(), w=()):
            return S.op("pe", fn, r, w)

        def DMA(out, in_, r, w, ds, slow=False, q=None):
            if q is None:
                q = "pool" if any(ds is x for x in POOLQ) else "sp"
            if slow:
                return S.op(q, lambda e: e.dma_start(out=out, in_=in_, allow_slow_non_contiguous=True), r, w, ds)
            return S.op(q, lambda e: e.dma_start(out=out, in_=in_), r, w, ds)

        def mm(out, lhsT, rhs, start, stop, r, w):
            return PE(lambda e: e.matmul(out, lhsT=lhsT, rhs=rhs, start=start, stop=stop), r, w)

        def tt_op(eng, out, in0, in1, op, r, w):
            return S.op(eng, lambda e: e.tensor_tensor(out=out, in0=in0, in1=in1, op=op), r, w)

        def bc(ap, shape, axis):
            return ap.unsqueeze(axis).to_broadcast(list(shape))

        DMA(CF[:], consts, [], ["CF"], d_c)
        Vv(lambda e: e.tensor_copy(out=identb[:], in_=CF[:, 0:128]), ["CF"], ["identb"])
        Vv(lambda e: e.tensor_copy(out=Jb[:], in_=CF[:, 128:256]), ["CF"], ["Jb"])
        for h in range(4):
            Vv(lambda e, h=h: e.tensor_copy(out=maskf4[:, 128 * h:128 * h + 128], in_=CF[:, 256:384]), ["CF"], ["maskf4"])
        Vv(lambda e: e.memset(ones128[:], 1.0 / 128), [], ["ones128"])
        Vv(lambda e: e.memset(ones_row[:], 1.0), [], ["ones_row"])
        identf = CF[:, 0:128]
        halfmask = CF[:, 384:386]
        rmask = CF[:, 386:898]
        DMA(FG[:], final_norm_g.partition_broadcast(128), [], ["FG"], d_c)
        for v in range(2):
            DMA(cT[:, :, v], cvec[v].rearrange("(k p) -> p k", p=128), [], ["cT"], d_c, slow=True)
        for l in range(2):
            DMA(bmodT[:, l, :], b_mod[l, 0:2048].rearrange("(j p) -> p j", p=128), [], ["bmodT"], d_c, slow=True)
        DMA(bmod_row[:], b_mod[:, 2048:3072].rearrange("(o l) c -> o l c", o=1), [], ["bmod_row"], d_c)
        for l in range(2):
            DMA(normgT[:, l, :], norm_g[l].rearrange("(k p) -> p k", p=128), [], ["normgT"], d_c, slow=True)
        for r_ in range(2):
            DMA(lbl[:, r_, :], lb_logits[r_].rearrange("(h p) -> p h", p=128), [], ["lbl"], d_c, slow=True)
        DMA(hg[:], hgrn_norm_g.rearrange("(h p) -> p h", p=128), [], ["hg"], d_c, slow=True)
        DMA(dT[:], s5_d.rearrange("(a p) -> p a", p=128), [], ["dT"], d_c, slow=True)
        DMA(bgluT[:], b_glu.rearrange("(a p) -> p a", p=128), [], ["bgluT"], d_c, slow=True)
        for j_ in range(3):
            DMA(cwT[:, j_, :], conv_w[j_].rearrange("(f p) -> p f", p=128), [], ["cwT"], d_c, slow=True)
        DMA(cbT[:], conv_b.rearrange("(f p) -> p f", p=128), [], ["cbT"], d_c, slow=True)

        A(lambda e: e.activation(out=csT[:], in_=cT[:], func=AF.Silu), ["cT"], ["csT"])
        Vv(lambda e: e.tensor_tensor(out=lb[:], in0=lbl[:, 0, :], in1=lbl[:, 1, :], op=ALU.subtract), ["lbl"], ["lb"])
        A(lambda e: e.activation(out=lb[:], in_=lb[:], func=AF.Sigmoid), ["lb"], ["lb"])
        Vv(lambda e: e.tensor_scalar(out=oml[:], in0=lb[:], scalar1=-1.0, scalar2=1.0, op0=ALU.mult, op1=ALU.add), ["lb"], ["oml"])
        Vv(lambda e: e.tensor_scalar(out=noml[:], in0=oml[:], scalar1=-1.0, scalar2=None, op0=ALU.mult), ["oml"], ["noml"])
        for a in range(4):
            Vv(lambda e, a=a: e.tensor_scalar(out=Ddiag[a][:], in0=identf, scalar1=dT[:, a:a + 1], scalar2=None, op0=ALU.mult),
               ["CF", "dT"], ["Ddiag%d" % a])

        Xs = X[:].rearrange("p t d -> p (t d)").rearrange("p (k c) -> p k c", c=512)
        XNs = XN[:].rearrange("p t d -> p (t d)").rearrange("p (k c) -> p k c", c=512)
        STG = [dict(f=Xs, b=XNs, fk=["X"], bk=["XN", "XNb", "XNc"], ds=d_stg),
               dict(f=WMM[:].rearrange("p (k c) -> p k c", c=512), b=hT[:], fk=["W", "MM"], bk=["hT", "hTb", "hTc"], ds=d_stg2)]
        stg_i = {"i": 0}

        def next_stg():
            s = STG[stg_i["i"] % 2]
            stg_i["i"] += 1
            return s

        for l in range(2 if setup >= 2 else 0):
            for pc in range(4):
                sg = next_stg()
                DMA(sg["f"], w_mod[l][:, 512 * pc:512 * pc + 512].rearrange("(k p) c -> p k c", p=128), [], sg["fk"], sg["ds"])
                for i in range(4):
                    for k in range(8):
                        mm(PS[0][:, 2 * i:2 * i + 2], sg["f"][:, k, 128 * i:128 * i + 128], csT[:, k, :], k == 0, k == 7,
                           sg["fk"] + ["csT"], ["ps0"])
                Vv(lambda e, l=l, pc=pc: e.tensor_tensor(
                    out=MODT[:, l, 4 * pc:4 * pc + 4, :], in0=PS[0][:, 0:8].rearrange("p (j v) -> p j v", v=2),
                    in1=bc(bmodT[:, l, 4 * pc:4 * pc + 4], [128, 4, 2], 2), op=ALU.add), ["ps0", "bmodT"], ["MODT"])
            Vv(lambda e, l=l: e.tensor_scalar(out=AT[:, l], in0=MODT[:, l, 8:16, :], scalar1=1.0, scalar2=None, op0=ALU.add),
               ["MODT"], ["AT"])
            Vv(lambda e, l=l: e.tensor_tensor(out=AT[:, l], in0=AT[:, l], in1=bc(normgT[:, l, :], [128, 8, 2], 2), op=ALU.mult),
               ["AT", "normgT"], ["AT"])

        def gate_bcast(l, v):
            for pc in range(2):
                sg = next_stg()
                DMA(sg["f"], w_mod[l][:, 2048 + 512 * pc:2048 + 512 * pc + 512].rearrange("(k p) c -> p k c", p=128), [], sg["fk"], sg["ds"])
                for k in range(8):
                    mm(PS[1][0:1, 0:512], csT[:, k, v:v + 1], sg["f"][:, k, :], k == 0, k == 7, sg["fk"] + ["csT"], ["ps1"])
                Vv(lambda e, pc=pc: e.tensor_tensor(out=grow[0:1, 512 * pc:512 * pc + 512], in0=PS[1][0:1, 0:512],
                                                    in1=bmod_row[0:1, l, 512 * pc:512 * pc + 512], op=ALU.add),
                   ["ps1", "bmod_row"], ["grow"])
            for pc in range(2):
                mm(PS[2][:, 0:512], ones_row[0:1, 0:128], grow[0:1, 512 * pc:512 * pc + 512], True, True,
                   ["ones_row", "grow"], ["ps2"])
                A(lambda e, pc=pc: e.activation(out=GB[:, 512 * pc:512 * pc + 512], in_=PS[2][:, 0:512], func=AF.Copy),
                  ["ps2"], ["XP"])

        def cast_piece(src_ap, units, gated_cols=None):
            sg = next_stg()
            Fs, Bs, fk, bk = sg["f"], sg["b"], sg["fk"], sg["bk"]
            DMA(Fs, src_ap.rearrange("(k p) c -> p k c", p=128), [], fk, sg["ds"])
            Bf = Bs.rearrange("p k c -> p (k c)")
            Bp = Bf.rearrange("p (u k c) -> p k u c", u=4, k=8)

            def src4(k0, k1):
                return Fs[:, k0:k1, :].rearrange("p k (u c) -> p k u c", u=4)

            if gated_cols is None:
                A(lambda e: e.activation(out=Bp[:, 0:3], in_=src4(0, 3), func=AF.Copy), fk, [bk[0]])
                Vv(lambda e: e.tensor_copy(out=Bp[:, 3:6], in_=src4(3, 6)), fk, [bk[1]])
                Pl(lambda e: e.tensor_copy(out=Bp[:, 6:8], in_=src4(6, 8)), fk, [bk[2]])
            else:
                g = GB[:, gated_cols:gated_cols + 512].rearrange("p (u c) -> p u c", u=4)
                Vv(lambda e: e.tensor_tensor(out=Bp[:, 0:6], in0=src4(0, 6), in1=bc(g, [128, 6, 4, 128], 1), op=ALU.mult),
                   fk + ["XP"], [bk[0], bk[1]])
                Pl(lambda e: e.tensor_tensor(out=Bp[:, 6:8], in0=src4(6, 8), in1=bc(g, [128, 2, 4, 128], 1), op=ALU.mult),
                   fk + ["XP"], [bk[2]])
            for i, u in enumerate(units):
                DMA(wscr[u], Bf[:, 1024 * i:1024 * i + 1024], bk, ["wscr%d" % u], d_wst)

        def cast_all():
            for pc in range(7 if setup >= 3 else 0):
                cast_piece(w_in_even[:, 512 * pc:512 * pc + 512], [4 * pc + i for i in range(4)])
            if setup >= 4:
                DMA(Xs[:, 0:4, :], w_glu.rearrange("(k p) c -> p k c", p=128), [], ["X"], d_stg)
                A(lambda e: e.activation(out=WG[:], in_=Xs[:, 0:4, :], func=AF.Copy), ["X"], ["WG"])
            for l, (wsrc, ubase) in enumerate([(w_out_even, U_EVOUT), (w_out_odd, U_ODOUT)] if setup >= 5 else []):
                for v in range(2):
                    gate_bcast(l, v)
                    for pc in range(2):
                        cast_piece(wsrc[:, 512 * pc:512 * pc + 512], [ubase + 8 * v + 4 * pc + i for i in range(4)], gated_cols=512 * pc)
            for pc in range(8 if setup >= 6 else 0):
                units = []
                for i in range(4):
                    ch = 4 * pc + i
                    j, f = ch // 8, ch % 8
                    units.append(U_ODIN + 4 * f + j)
                cast_piece(w_in_odd[:, 512 * pc:512 * pc + 512], units)


        def s5_consts(d):
            lre, lim, ldt, dt, ar, th, m_, cc, sn, t1, t2, t3, Lr, Li, cre, cim = c16

            def T(fn, r, w):
                return Vv(fn, r, w)

            for gl in range(2):
                DMA(lre[64 * gl:64 * gl + 64, :], s5_lam_re[d].rearrange("(P two) p -> two p P", two=2)[gl], [], ["lre"], d_c, slow=True)
                DMA(lim[64 * gl:64 * gl + 64, :], s5_lam_im[d].rearrange("(P two) p -> two p P", two=2)[gl], [], ["lim"], d_c, slow=True)
                DMA(ldt[64 * gl:64 * gl + 64, :], s5_log_dt[d].rearrange("(P two) -> two P", two=2)[gl].partition_broadcast(64),
                    [], ["ldt"], d_c, slow=True)
                DMA(Bst[0][64 * gl:64 * gl + 64, :, :], s5_b_re[d].rearrange("(P two) p s -> two p P s", two=2)[gl], [], ["Bst0"], d_c)
                DMA(Bst[1][64 * gl:64 * gl + 64, :, :], s5_b_im[d].rearrange("(P two) p s -> two p P s", two=2)[gl], [], ["Bst1"], d_c)
            if s5lvl < 2:
                return
            A(lambda e: e.activation(out=dt[:], in_=ldt[:], func=AF.Exp), ["ldt"], ["dt"])
            T(lambda e: e.tensor_tensor(out=ar[:], in0=lre[:], in1=dt[:], op=ALU.mult), ["lre", "dt"], ["ar"])
            T(lambda e: e.tensor_tensor(out=th[:], in0=lim[:], in1=dt[:], op=ALU.mult), ["lim", "dt"], ["th"])
            A(lambda e: e.activation(out=m_[:], in_=ar[:], func=AF.Exp), ["ar"], ["m_"])
            A(lambda e: e.activation(out=sn[:], in_=th[:], func=AF.Sin, scale=1.0 / 16), ["th"], ["sn"])
            A(lambda e: e.activation(out=cc[:], in_=th[:], func=AF.Sin, scale=1.0 / 16, bias=float(np.pi / 2)), ["th"], ["cc"])

            def csq(c_, s_):
                T(lambda e: e.tensor_tensor(out=t1[:], in0=c_[:], in1=c_[:], op=ALU.mult), ["cc", "sn"], ["t1"])
                T(lambda e: e.tensor_tensor(out=t2[:], in0=s_[:], in1=s_[:], op=ALU.mult), ["cc", "sn"], ["t2"])
                T(lambda e: e.tensor_tensor(out=t3[:], in0=c_[:], in1=s_[:], op=ALU.mult), ["cc", "sn"], ["t3"])
                T(lambda e: e.tensor_tensor(out=c_[:], in0=t1[:], in1=t2[:], op=ALU.subtract), ["t1", "t2"], ["cc"])
                T(lambda e: e.tensor_scalar(out=s_[:], in0=t3[:], scalar1=2.0, scalar2=None, op0=ALU.mult), ["t3"], ["sn"])

            for _ in range(4):
                csq(cc, sn)
            T(lambda e: e.tensor_tensor(out=Lr[:], in0=m_[:], in1=cc[:], op=ALU.mult), ["m_", "cc"], ["Lr"])
            T(lambda e: e.tensor_tensor(out=Li[:], in0=m_[:], in1=sn[:], op=ALU.mult), ["m_", "sn"], ["Li"])
            T(lambda e: e.tensor_tensor(out=t1[:], in0=lre[:], in1=lre[:], op=ALU.mult), ["lre"], ["t1"])
            T(lambda e: e.tensor_tensor(out=t2[:], in0=lim[:], in1=lim[:], op=ALU.mult), ["lim"], ["t2"])
            T(lambda e: e.tensor_tensor(out=t1[:], in0=t1[:], in1=t2[:], op=ALU.add), ["t1", "t2"], ["t1"])
            T(lambda e: e.reciprocal(out=t1[:], in_=t1[:]), ["t1"], ["t1"])
            T(lambda e: e.tensor_scalar(out=t2[:], in0=Lr[:], scalar1=-1.0, scalar2=None, op0=ALU.add), ["Lr"], ["t2"])
            T(lambda e: e.tensor_tensor(out=cre[:], in0=t2[:], in1=lre[:], op=ALU.mult), ["t2", "lre"], ["cre"])
            T(lambda e: e.tensor_tensor(out=t3[:], in0=Li[:], in1=lim[:], op=ALU.mult), ["Li", "lim"], ["t3"])
            T(lambda e: e.tensor_tensor(out=cre[:], in0=cre[:], in1=t3[:], op=ALU.add), ["cre", "t3"], ["cre"])
            T(lambda e: e.tensor_tensor(out=cre[:], in0=cre[:], in1=t1[:], op=ALU.mult), ["cre", "t1"], ["cre"])
            T(lambda e: e.tensor_tensor(out=cim[:], in0=Li[:], in1=lre[:], op=ALU.mult), ["Li", "lre"], ["cim"])
            T(lambda e: e.tensor_tensor(out=t3[:], in0=t2[:], in1=lim[:], op=ALU.mult), ["t2", "lim"], ["t3"])
            T(lambda e: e.tensor_tensor(out=cim[:], in0=cim[:], in1=t3[:], op=ALU.subtract), ["cim", "t3"], ["cim"])
            T(lambda e: e.tensor_tensor(out=cim[:], in0=cim[:], in1=t1[:], op=ALU.mult), ["cim", "t1"], ["cim"])
            if s5lvl < 3:
                return
            crb = bc(cre[:], [128, 16, 16], 2)
            cib = bc(cim[:], [128, 16, 16], 2)
            T(lambda e: e.tensor_tensor(out=Bbar[0][:], in0=Bst[0][:], in1=crb, op=ALU.mult), ["Bst0", "cre"], ["Bbar0"])
            T(lambda e: e.tensor_tensor(out=Btmp[:], in0=Bst[1][:], in1=cib, op=ALU.mult), ["Bst1", "cim"], ["Btmp"])
            T(lambda e: e.tensor_tensor(out=Bbar[0][:], in0=Bbar[0][:], in1=Btmp[:], op=ALU.subtract), ["Bbar0", "Btmp"], ["Bbar0"])
            T(lambda e: e.tensor_tensor(out=Bbar[1][:], in0=Bst[1][:], in1=crb, op=ALU.mult), ["Bst1", "cre"], ["Bbar1"])
            T(lambda e: e.tensor_tensor(out=Btmp[:], in0=Bst[0][:], in1=cib, op=ALU.mult), ["Bst0", "cim"], ["Btmp"])
            T(lambda e: e.tensor_tensor(out=Bbar[1][:], in0=Bbar[1][:], in1=Btmp[:], op=ALU.add), ["Bbar1", "Btmp"], ["Bbar1"])
            if s5lvl < 4:
                return
            hm = bc(halfmask, [128, 2, 16], 2)
            for a in range(4):
                for part in range(2):
                    for q in range(4):
                        P = 4 * a + q
                        T(lambda e, q=q, P=P, part=part: e.tensor_tensor(
                            out=M4[q][:, 32 * q:32 * q + 32].rearrange("p (g s) -> p g s", s=16),
                            in0=bc(Bbar[part][:, P, :], [128, 2, 16], 1), in1=hm, op=ALU.mult),
                          ["Bbar%d" % part, "CF"], ["M4_%d" % q])
                        mm(PS[3][:, 0:128], M4[q][:], identb[:], q == 0, q == 3, ["M4_%d" % q, "identb"], ["ps3"])
                    A(lambda e, a=a, part=part: e.activation(out=BTt[a][part][:], in_=PS[3][:, 0:128], func=AF.Copy),
                      ["ps3"], ["BTt"])
                    A(lambda e, a=a, part=part: e.activation(out=BTt3[a][part][64:128, :], in_=PS[3][64:128, 0:128], func=AF.Copy),
                      ["ps3"], ["BTt"])
                    Vv(lambda e, a=a, part=part: e.memset(BTt3[a][part][64:96, :], 0.0), ["BTt"], ["BTt"])
            if s5lvl < 5:
                return
            for a in range(4):
                for part in range(2):
                    src = (s5_c_re if part == 0 else s5_c_im)
                    ci = (2 * a + part) % 2
                    Cpad = Cpads[ci]
                    ck = ["Cpad%d_%d" % (ci, g8) for g8 in range(8)]
                    for g8 in range(8):
                        DMA(Cpad[16 * g8:16 * g8 + 16, 64 * (g8 % 2):64 * (g8 % 2) + 64], src[d, 8 * a + g8], [], [ck[g8]], d_c)
                    if s5lvl < 5.5:
                        continue
                    Vv(lambda e, Cpad=Cpad: e.tensor_copy(out=Cpb[:], in_=Cpad[:]), ck, ["Cpb"])
                    mm(PS[3][:, 128:256], Cpb[:], identb[:], True, True, ["Cpb", "identb"], ["ps3b"])
                    if s5lvl < 5.7:
                        continue
                    A(lambda e, a=a, part=part: e.activation(out=Cst[a][part][:], in_=PS[3][:, 128:256], func=AF.Copy,
                                                             scale=(1.0 if part == 0 else -1.0)), ["ps3b"], ["Cst"])
                    A(lambda e, a=a, part=part: e.activation(out=Cst3[a][part][:, 32:64], in_=PS[3][:, 224:256], func=AF.Copy,
                                                             scale=(1.0 if part == 0 else -1.0)), ["ps3b"], ["Cst"])
            if s5lvl < 6:
                return
            Vv(lambda e: e.memset(Tc[:, :, 0:1], 1.0), [], ["Tc"])
            Vv(lambda e: e.memset(Ts[:, :, 0:1], 0.0), [], ["Ts"])
            for kk in range(6):
                n = 1 << kk
                ucb = bc(cc[:], [128, 16, n], 2)
                usb = bc(sn[:], [128, 16, n], 2)
                ta_, tb_ = Ttmp[0][:, :, 0:n], Ttmp[1][:, :, 0:n]
                T(lambda e, n=n, ucb=ucb, ta_=ta_: e.tensor_tensor(out=ta_, in0=Tc[:, :, 0:n], in1=ucb, op=ALU.mult), ["Tc", "cc"], ["XP"])
                T(lambda e, n=n, usb=usb, tb_=tb_: e.tensor_tensor(out=tb_, in0=Ts[:, :, 0:n], in1=usb, op=ALU.mult), ["Ts", "sn"], ["XP"])
                T(lambda e, n=n, ta_=ta_, tb_=tb_: e.tensor_tensor(out=Tc[:, :, n:2 * n], in0=ta_, in1=tb_, op=ALU.subtract),
                  ["XP", "XP", "Tc"], ["Tc"])
                T(lambda e, n=n, usb=usb, ta_=ta_: e.tensor_tensor(out=ta_, in0=Tc[:, :, 0:n], in1=usb, op=ALU.mult), ["Tc", "sn"], ["XP"])
                T(lambda e, n=n, ucb=ucb, tb_=tb_: e.tensor_tensor(out=tb_, in0=Ts[:, :, 0:n], in1=ucb, op=ALU.mult), ["Ts", "cc"], ["XP"])
                T(lambda e, n=n, ta_=ta_, tb_=tb_: e.tensor_tensor(out=Ts[:, :, n:2 * n], in0=ta_, in1=tb_, op=ALU.add),
                  ["XP", "XP", "Ts"], ["Ts"])
                if kk < 5:
                    csq(cc, sn)
            Vv(lambda e: e.memset(Mg[:, :, 0:1], 0.0), [], ["Mg"])
            Vv(lambda e: e.tensor_copy(out=Mg[:, :, 1:64], in_=bc(m_[:], [128, 16, 63], 2)), ["m_", "Mg"], ["Mg"])
            Vv(lambda e: e.tensor_copy(out=LL1[:], in_=bc(Lr[:], [128, 2, 16], 1)), ["Lr"], ["LL1"])
            Vv(lambda e: e.tensor_scalar(out=LL2[:, 0, :], in0=Li[:], scalar1=-1.0, scalar2=None, op0=ALU.mult), ["Li"], ["LL2"])
            Vv(lambda e: e.tensor_copy(out=LL2[:, 1, :], in_=Li[:]), ["Li", "LL2"], ["LL2"])
            T(lambda e: e.tensor_tensor(out=t1[:], in0=Lr[:], in1=Tc[:, :, 63], op=ALU.mult), ["Lr", "Tc"], ["t1"])
            T(lambda e: e.tensor_tensor(out=t2[:], in0=Li[:], in1=Ts[:, :, 63], op=ALU.mult), ["Li", "Ts"], ["t2"])
            T(lambda e: e.tensor_tensor(out=t1[:], in0=t1[:], in1=t2[:], op=ALU.subtract), ["t1", "t2"], ["t1"])
            T(lambda e: e.tensor_tensor(out=t2[:], in0=Lr[:], in1=Ts[:, :, 63], op=ALU.mult), ["Lr", "Ts", "t1"], ["t2"])
            T(lambda e: e.tensor_tensor(out=t3[:], in0=Li[:], in1=Tc[:, :, 63], op=ALU.mult), ["Li", "Tc"], ["t3"])
            T(lambda e: e.tensor_tensor(out=t2[:], in0=t2[:], in1=t3[:], op=ALU.add), ["t2", "t3"], ["t2"])
            Vv(lambda e: e.tensor_copy(out=LG1[:], in_=bc(t1[:], [128, 2, 16], 1)), ["t1"], ["LG1"])
            Vv(lambda e: e.tensor_scalar(out=LG2[:, 0, :], in0=t2[:], scalar1=-1.0, scalar2=None, op0=ALU.mult), ["t2"], ["LG2"])
            Vv(lambda e: e.tensor_copy(out=LG2[:, 1, :], in_=t2[:]), ["t2", "LG2"], ["LG2"])
            A(lambda e: e.activation(out=Tcb[:], in_=Tc[:], func=AF.Copy), ["Tc"], ["Tcb"])
            A(lambda e: e.activation(out=Tsb[:], in_=Ts[:], func=AF.Copy), ["Ts"], ["Tsb"])

        ring = {"pos": 0}

        def load_unit(u):
            slot = ring["pos"] % NSLOT
            ring["pos"] += 1
            DMA(WR[slot][:].rearrange("p k c -> p (k c)"), wscr[u], ["wscr%d" % u], ["wr%d" % slot], d_slot[slot],
                q=("pool" if u >= U_ODIN else "sp"))
            return slot

        def proj_fm(u, bank, src="hT"):
            slot = load_unit(u)
            for k in range(8):
                mm(PS[bank][:, 0:512], WR[slot][:, k, :], hT[:, k, :], k == 0, k == 7, ["wr%d" % slot, src], ["ps%d" % bank])

        Xf = hT[:].rearrange("p k c -> p (k c)").bitcast(F32).rearrange("p (t d) -> p t d", t=2)
        HK = ["hT", "hTb", "hTc"]
        XK = ["XN", "XNb", "XNc"]

        def front_gen(blk, l, vec, rev, stream):
            for t in range(4):
                if stream:
                    xin = Xf[:, t % 2, :]
                    xk = HK
                    DMA(xin, xs[512 * blk + 128 * t:512 * blk + 128 * t + 128, :], [], HK, d_xf)
                else:
                    xin = X[:, t, :]
                    xk = ["X"]
                A(lambda e, t=t, xin=xin: e.activation(out=XN[:, t, :], in_=xin, func=AF.Square, accum_out=ss[:, t:t + 1]),
                  xk, XK + ["ss"])
                A(lambda e, t=t: e.activation(out=rstd[:, t:t + 1], in_=ss[:, t:t + 1], func=AF.Ln, scale=1.0 / 1024, bias=EPS),
                  ["ss"], ["rstd"])
                A(lambda e, t=t: e.activation(out=rstd[:, t:t + 1], in_=rstd[:, t:t + 1], func=AF.Exp, scale=-0.5), ["rstd"], ["rstd"])
                A(lambda e, t=t, xin=xin: e.activation(out=XN[:, t, :], in_=xin, func=AF.Copy, scale=rstd[:, t:t + 1]),
                  xk + ["rstd"], XK)
                yield
            perm = Jb if rev else identb
            for k in range(8):
                bank = k % 2
                for t in range(4):
                    pos = 3 - t if rev else t
                    mm(PS[bank][:, 128 * pos:128 * pos + 128], XN[:, t, 128 * k:128 * k + 128], perm[:], True, True,
                       ["XN", "Jb", "identb"], ["ps%d" % bank])
                A(lambda e, k=k, bank=bank: e.activation(out=hT[:, k, :], in_=PS[bank][:, 0:512], func=AF.Identity,
                                                         scale=AT[:, l, k, vec:vec + 1], bias=MODT[:, l, k, vec:vec + 1]),
                  ["ps%d" % bank, "AT", "MODT"], HK)
                if k % 2 == 1:
                    yield

        def front(blk, l, vec, rev, stream):
            for _ in front_gen(blk, l, vec, rev, stream):
                pass

        def even_prep(blk, d, sw, bs):
            rev = (d == 1)
            vec = 0 if blk < 8 else 1
            V_, uT_, Qt_, Kt_, EB_ = bs["V"], bs["uT"], bs["Qt"], bs["Kt"], bs["EB"]
            kV, uk, qk, kk, kEB = bs["kV"], bs["uk"], bs["qk"], bs["kk"], bs["kEB"]
            yield from front_gen(blk, 0, vec, rev, True)
            yield "FRONT"
            for a in range(4):
                proj_fm(20 + a, a % 4)
                A(lambda e, a=a: e.activation(out=uT_[a], in_=PS[a % 4][:, 0:512], func=AF.Copy), ["ps%d" % (a % 4)], [uk[a]])
                yield
            yield "READY"
            if sw == 2:
                for a in range(4):
                    proj_fm(24 + a, a % 4)
                    A(lambda e, a=a: e.activation(out=GBs[a], in_=PS[a % 4][:, 0:512], func=AF.Silu), ["ps%d" % (a % 4)], ["GBs%d" % a])
                    yield
            for i in range(4):
                slot = load_unit(12 + i)
                bank = i % 4
                for t in range(4):
                    for k in range(8):
                        mm(PS[bank][:, 128 * t:128 * t + 128], hT[:, k, 128 * t:128 * t + 128], WR[slot][:, k, :], k == 0, k == 7,
                           ["wr%d" % slot, "hT"], ["ps%d" % bank])
                A(lambda e, i=i, bank=bank: e.activation(out=V_[:, :, 128 * i:128 * i + 128],
                                                         in_=PS[bank][:, 0:512].rearrange("p (t c) -> p t c", c=128), func=AF.Copy),
                  ["ps%d" % bank], [kV])
                yield
            tsets = [(TT, ["ta", "tb", "tc", "td", "te"]), (bs["T2"], bs["T2k"])]
            two_sets = bs["T2"] is not TT

            def st_proj(i, h):
                proj_fm(h, 2 * i)
                proj_fm((8 if d == 1 else 4) + h, 2 * i + 1)

            def st_gate(i, h):
                (ta, tb, tc_, td, te), (ka, kb, kc, kd, ke) = tsets[i]
                A(lambda e, i=i, ta=ta: e.activation(out=ta[:], in_=PS[2 * i + 1][:, 0:512], func=AF.Sigmoid), ["ps%d" % (2 * i + 1)], [ka])
                A(lambda e, h=h, ta=ta, tb=tb: e.activation(out=tb[:], in_=ta[:], func=AF.Ln, scale=oml[:, h:h + 1], bias=lb[:, h:h + 1]),
                  [ka, "oml", "lb"], [kb])
                A(lambda e, h=h, ta=ta: e.activation(out=ta[:], in_=ta[:], func=AF.Identity, scale=noml[:, h:h + 1], bias=oml[:, h:h + 1]),
                  [ka, kb, "oml", "noml"], [ka])

            def st_scan(i, h):
                (ta, tb, tc_, td, te), (ka, kb, kc, kd, ke) = tsets[i]
                Vv(lambda e, tb=tb, tc_=tc_: e.tensor_tensor_scan(out=tc_[:], data0=rmask, data1=tb[:], initial=0.0, op0=ALU.mult, op1=ALU.add),
                   [kb, "CF"], [kc])

            def st_exp(i, h):
                (ta, tb, tc_, td, te), (ka, kb, kc, kd, ke) = tsets[i]
                A(lambda e, tc_=tc_, td=td: e.activation(out=td[:], in_=tc_[:], func=AF.Exp), [kc], [kd])
                A(lambda e, tc_=tc_, te=te: e.activation(out=te[:], in_=tc_[:], func=AF.Exp, scale=-1.0), [kc], [ke])
                A(lambda e, h=h, tc_=tc_: e.activation(out=EB_[:, :, h], in_=tc_[:].rearrange("p (c j) -> p c j", j=64)[:, :, 63], func=AF.Exp),
                  [kc], [kEB])

            def st_qk(i, h):
                (ta, tb, tc_, td, te), (ka, kb, kc, kd, ke) = tsets[i]
                Vv(lambda e, h=h, i=i, td=td: e.tensor_tensor(out=Qt_[h], in0=PS[2 * i][:, 0:512], in1=td[:], op=ALU.mult),
                   ["ps%d" % (2 * i), kd], [qk[h]])
                Vv(lambda e, h=h, ta=ta, te=te: e.tensor_tensor(out=Kt_[h], in0=ta[:], in1=te[:], op=ALU.mult), [ka, ke], [kk[h]])

            for hp in range(2):
                hs_ = [(0, 2 * hp), (1, 2 * hp + 1)]
                for i, h in hs_:
                    st_proj(i, h)
                if two_sets:
                    for st in (st_gate, st_scan, st_exp, st_qk):
                        for i, h in hs_:
                            st(i, h)
                else:
                    for i, h in hs_:
                        for st in (st_gate, st_scan, st_exp, st_qk):
                            st(i, h)
                yield
            if sw == 2:
                for h in range(4):
                    proj_fm(16 + h, h % 4)
                    A(lambda e, h=h: e.activation(out=GA[h], in_=PS[h % 4][:, 0:512], func=AF.Silu), ["ps%d" % (h % 4)], ["GA%d" % h])
                    yield

        def even_tiles(blk, d, sw, bs, gen=None, npull=3, prep_rest=None, ppull=4, on_token=None):
            rev = (d == 1)
            V_, uT_, Qt_, Kt_, EB_ = bs["V"], bs["uT"], bs["Qt"], bs["Kt"], bs["EB"]
            kV, uk, qk, kk, kEB = bs["kV"], bs["uk"], bs["qk"], bs["kk"], bs["kEB"]
            first_blk = 0 if not rev else 7

            def is_start(t):
                return (blk == first_blk and t == 0) or (blk >= 8 and t % 2 == 0)

            def is_end(t):
                return blk >= 8 and t % 2 == 1

            def seq_of(t):
                at = t if not rev else 3 - t
                return 2 * (blk - 8) + (at // 2)

            def hgrn_tile(t):
                cols = slice(128 * t, 128 * t + 128)
                if sw == 2:
                    DMA(OBT[:, t % 2, :], obscr[blk * 4 + (3 - t)], ["obscr%d" % (blk * 4 + 3 - t)], ["OBT%d" % (t % 2)], d_obl)
                if is_start(t):
                    if blk >= 8:
                        Pl(lambda e: e.memset(S32[:], 0.0), [], ["S32"])
                    else:
                        DMA(S32[:], st_h[d].rearrange("h k v -> k h v"), [], ["S32"], d_st)
                ab = Abuf[t % 2]
                abk = "Abuf%d" % (t % 2)
                for h in range(4):
                    mm(PS[4][:, 128 * h:128 * h + 128], Kt_[h][:, cols], Qt_[h][:, cols], True, True, [kk[h], qk[h]], ["ps4"])
                Vv(lambda e, ab=ab: e.tensor_tensor(out=ab[:], in0=PS[4][:, 0:512], in1=maskf4[:], op=ALU.mult), ["ps4", "maskf4"], [abk])
                for h in range(4):
                    mm(PS[6][:, 128 * h:128 * h + 128], Kt_[h][:, cols], identb[:], True, True, [kk[h], "identb"], ["ps6"])
                A(lambda e, t=t: e.activation(out=Ktok[:, t, :], in_=PS[6][:, 0:512], func=AF.Copy), ["ps6"], ["Ktok"])
                sins = []
                for c in range(2):
                    gc = 2 * t + c
                    rows = slice(64 * c, 64 * c + 64)
                    for h in range(4):
                        mm(PS[6][:, 128 * h:128 * h + 128], Ktok[rows, t, 128 * h:128 * h + 128], V_[rows, t, 128 * h:128 * h + 128],
                           True, True, ["Ktok", kV], ["ps6"])
                    si = (2 * t + c) % 4
                    sins.append(si)
                    A(lambda e, si=si: e.activation(out=Sin[si][:], in_=S32[:], func=AF.Copy), ["S32"], ["Sin%d" % si])
                    Vv(lambda e: e.tensor_tensor(out=S32[:].rearrange("p h v -> p (h v)"), in0=S32[:].rearrange("p h v -> p (h v)"),
                                                 in1=PS[6][:, 0:512], op=ALU.add), ["S32", "ps6"], ["S32"])
                    Vv(lambda e, gc=gc: e.tensor_tensor(out=S32[:], in0=S32[:], in1=bc(EB_[:, gc, :], [128, 4, 128], 2), op=ALU.mult),
                       ["S32", kEB], ["S32"])
                if is_end(t):
                    DMA(nh[seq_of(t), d].rearrange("h k v -> k h v"), S32[:], ["S32"], [], d_so)
                if sw == 2:
                    for h in range(4):
                        hs = slice(128 * h, 128 * h + 128)
                        mm(PS[7][:, hs], V_[:, t, hs], ab[:, hs], True, False, [kV, abk], ["ps7"])
                        for c in range(2):
                            mm(PS[7][:, 128 * h + 64 * c:128 * h + 64 * c + 64], Sin[sins[c]][:, h, :],
                               Qt_[h][:, 128 * t + 64 * c:128 * t + 64 * c + 64], False, False, ["Sin%d" % sins[c], qk[h]], ["ps7"])
                        mm(PS[7][:, hs], OBT[:, t % 2, hs], Jb[:], False, True, ["OBT%d" % (t % 2), "Jb"], ["ps7"])
                    A(lambda e: e.activation(out=osb[:], in_=PS[7][:, 0:512], func=AF.Copy), ["ps7"], ["te"])
                    A(lambda e: e.activation(out=osq[:], in_=PS[7][:, 0:512], func=AF.Square), ["ps7"], ["osq"])
                    for h in range(4):
                        hs = slice(128 * h, 128 * h + 128)
                        mm(PS[4][:, hs], ones128[:], osq[:, hs], True, True, ["ones128", "osq"], ["ps4"])
                    A(lambda e: e.activation(out=TT[0][:], in_=PS[4][:, 0:512], func=AF.Ln, bias=EPS), ["ps4"], ["ta"])
                    A(lambda e: e.activation(out=TT[0][:], in_=TT[0][:], func=AF.Exp, scale=-0.5), ["ta"], ["ta"])
                    Vv(lambda e: e.tensor_tensor(out=osb[:], in0=osb[:], in1=TT[0][:], op=ALU.mult), ["te", "ta"], ["te"])
                    for h in range(4):
                        Vv(lambda e, h=h, cols=cols: e.scalar_tensor_tensor(out=OA[h][:, cols], in0=osb[:, 128 * h:128 * h + 128], scalar=hg[:, h:h + 1],
                                                                            in1=GA[h][:, cols], op0=ALU.mult, op1=ALU.mult),
                           ["te", "hg", "GA%d" % h], ["OA%d" % h])
                else:
                    for h in range(4):
                        hs = slice(128 * h, 128 * h + 128)
                        mm(PS[7][:, hs], ab[:, hs], V_[:, t, hs], True, False, [abk, kV], ["ps7"])
                        for c in range(2):
                            mm(PS[7][64 * c:64 * c + 64, hs], Qt_[h][:, 128 * t + 64 * c:128 * t + 64 * c + 64], Sin[sins[c]][:, h, :],
                               False, True, ["Sin%d" % sins[c], qk[h]], ["ps7"])
                    ob = OST[0]
                    A(lambda e, ob=ob: e.activation(out=ob[:], in_=PS[7][:, 0:512], func=AF.Copy), ["ps7"], ["OST0"])
                    DMA(obscr[blk * 4 + t], ob[:], ["OST0"], ["obscr%d" % (blk * 4 + t)], d_ob)

            MMb = [MM, MM]
            MMk = ["MM", "MM"]

            Wb = [W, W1[:]]
            Wk = ["W", "W1"]

            def s5_A1(t, c):
                gc = 2 * t + c
                Wt, wkey = Wb[gc % 2], Wk[gc % 2]
                cc_ = slice(128 * t + 64 * c, 128 * t + 64 * c + 64)
                for a in range(4):
                    for part in range(2):
                        oc_ = slice(64 * (2 * a + part), 64 * (2 * a + part) + 64)
                        for q in range(4):
                            if q < 3:
                                mm(PS[q][:, oc_], BTt[a][part][32 * q:32 * q + 32, :],
                                   uT_[a][32 * q:32 * q + 32, cc_], True, True, ["BTt", uk[a]], ["ps%d" % q])
                            else:
                                mm(PS[q][:, oc_], BTt3[a][part][64:128, :],
                                   uT_[a][64:128, cc_], True, True, ["BTt", uk[a]], ["ps%d" % q])
                Wv = Wt.rearrange("p r (a q) j -> p r a q j", q=4)
                for q in range(4):
                    A(lambda e, q=q, Wv=Wv: e.activation(out=Wv[:, :, :, q, :].rearrange("p r a j -> p a r j"),
                                                         in_=PS[q][:, 0:512].rearrange("p (a r j) -> p a r j", a=4, r=2),
                                                         func=AF.Copy), ["ps%d" % q], [wkey])

            def s5_mod(t, c):
                gc = 2 * t + c
                Wt, wkey = Wb[gc % 2], Wk[gc % 2]
                mmv, mk = MMb[gc % 2], MMk[gc % 2]
                wk = [mk] if mk == "MM" else ["XN", "XNb", "XNc"]
                Vv(lambda e, mmv=mmv, Wt=Wt: e.tensor_tensor(out=mmv[:, 0], in0=Wt[:, 0], in1=Tc[:], op=ALU.mult), [wkey, "Tc"], wk)
                Vv(lambda e, mmv=mmv, Wt=Wt: e.tensor_tensor(out=mmv[:, 1], in0=Wt[:, 1], in1=Tc[:], op=ALU.mult), [wkey, "Tc"], wk)
                Vv(lambda e, Wt=Wt: e.tensor_tensor(out=Wt[:, 0], in0=Wt[:, 0], in1=Ts[:], op=ALU.mult), [wkey, "Ts"], [wkey])
                Vv(lambda e, Wt=Wt: e.tensor_tensor(out=Wt[:, 1], in0=Wt[:, 1], in1=Ts[:], op=ALU.mult), [wkey, "Ts"], [wkey])
                Vv(lambda e, mmv=mmv, Wt=Wt: e.tensor_tensor(out=mmv[:, 1], in0=mmv[:, 1], in1=Wt[:, 0], op=ALU.subtract), [wkey] + wk, wk)
                Vv(lambda e, mmv=mmv, Wt=Wt: e.tensor_tensor(out=mmv[:, 0], in0=mmv[:, 0], in1=Wt[:, 1], op=ALU.add), [wkey] + wk, wk)

            def final_state(t):
                if True:
                    seq = seq_of(t)
                    Vv(lambda e: e.tensor_tensor(out=Zo[:, 0, :], in0=Z[:, 0, :], in1=Tc[:, :, 63], op=ALU.mult), ["Z", "Tc"], ["Zo"])
                    Vv(lambda e: e.tensor_tensor(out=zt[1][:, 0, :], in0=Z[:, 1, :], in1=Ts[:, :, 63], op=ALU.mult), ["Z", "Ts"], ["zt1"])
                    Vv(lambda e: e.tensor_tensor(out=Zo[:, 0, :], in0=Zo[:, 0, :], in1=zt[1][:, 0, :], op=ALU.subtract), ["Zo", "zt1"], ["Zo"])
                    Vv(lambda e: e.tensor_tensor(out=Zo[:, 1, :], in0=Z[:, 0, :], in1=Ts[:, :, 63], op=ALU.mult), ["Z", "Ts", "Zo"], ["Zo"])
                    Vv(lambda e: e.tensor_tensor(out=zt[1][:, 0, :], in0=Z[:, 1, :], in1=Tc[:, :, 63], op=ALU.mult), ["Z", "Tc", "Zo"], ["zt1"])
                    Vv(lambda e: e.tensor_tensor(out=Zo[:, 1, :], in0=Zo[:, 1, :], in1=zt[1][:, 0, :], op=ALU.add), ["Zo", "zt1"], ["Zo"])
                    for gl in range(2):
                        DMA(nre[seq, d].rearrange("(P two) p -> two p P", two=2)[gl], Zo[64 * gl:64 * gl + 64, 0, :], ["Zo"], [], d_so, slow=True)
                        DMA(nim[seq, d].rearrange("(P two) p -> two p P", two=2)[gl], Zo[64 * gl:64 * gl + 64, 1, :], ["Zo"], [], d_so, slow=True)

            def s5_chain(t, c):
                gc = 2 * t + c
                mmv, mk = MMb[gc % 2], MMk[gc % 2]
                wk = [mk] if mk == "MM" else ["XN", "XNb", "XNc"]
                if c == 0 and is_start(t):
                    if blk >= 8:
                        Vv(lambda e: e.memset(Z[:], 0.0), [], ["Z"])
                    else:
                        for gl in range(2):
                            DMA(Z[64 * gl:64 * gl + 64, 0, :], st_re[d].rearrange("(P two) p -> two p P", two=2)[gl], [], ["Z"], d_st, slow=True)
                            DMA(Z[64 * gl:64 * gl + 64, 1, :], st_im[d].rearrange("(P two) p -> two p P", two=2)[gl], [], ["Z"], d_st, slow=True)
                        Vv(lambda e: e.tensor_copy(out=Z[:, 2, :], in_=Z[:, 0, :]), ["Z"], ["Z"])
                if eblvl < 5.3:
                    return
                first = (c == 0 and is_start(t))
                L1, L2 = (LL1, LL2) if first else (LG1, LG2)
                lk = ["LL1", "LL2"] if first else ["LG1", "LG2"]
                Vv(lambda e, L1=L1: e.tensor_tensor(out=zt[0][:], in0=L1[:], in1=Z[:, 0:2, :], op=ALU.mult), [lk[0], "Z"], ["zt0"])
                Vv(lambda e, L2=L2: e.tensor_tensor(out=zt[1][:], in0=L2[:], in1=Z[:, 1:3, :], op=ALU.mult), [lk[1], "Z"], ["zt1"])
                Vv(lambda e: e.tensor_tensor(out=zt[0][:], in0=zt[0][:], in1=zt[1][:], op=ALU.add), ["zt0", "zt1"], ["zt0"])
                Vv(lambda e, mmv=mmv: e.tensor_tensor(out=mmv[:, :, :, 0], in0=mmv[:, :, :, 0], in1=zt[0][:], op=ALU.add), wk + ["zt0"], wk)
                for part in range(2):
                    Vv(lambda e, part=part, mmv=mmv: e.tensor_tensor_scan(
                        out=XP[:, part].rearrange("p a j -> p (a j)"), data0=Mg[:].rearrange("p a j -> p (a j)"),
                        data1=mmv[:, part].rearrange("p a j -> p (a j)"), initial=0.0, op0=ALU.mult, op1=ALU.add),
                       wk + ["Mg"], ["XP"])
                Vv(lambda e: e.tensor_copy(out=Z[:, 0:2, :], in_=XP[:, :, :, 63]), ["XP"], ["Z"])
                Vv(lambda e: e.tensor_copy(out=Z[:, 2, :], in_=XP[:, 0, :, 63]), ["XP", "Z"], ["Z"])
                A(lambda e: e.activation(out=XPb[:], in_=XP[:], func=AF.Copy), ["XP"], ["XPb"])
                if c == 1 and is_end(t):
                    final_state(t)

            def load_ybt(t):
                if sw == 2:
                    DMA(YBT[:, t % 2, :], ybscr[blk * 4 + (3 - t)], ["ybscr%d" % (blk * 4 + 3 - t)], ["YBT%d" % (t % 2)], d_ybl)

            def s5_demod(t, c):
                Vv(lambda e: e.tensor_tensor(out=XH[:, 0], in0=XPb[:, 0], in1=Tcb[:], op=ALU.mult), ["XPb", "Tcb"], ["XH"])
                Vv(lambda e: e.tensor_tensor(out=ftmp[:], in0=XPb[:, 1], in1=Tsb[:], op=ALU.mult), ["XPb", "Tsb"], ["ftmp"])
                Vv(lambda e: e.tensor_tensor(out=XPb[:, 0], in0=XPb[:, 0], in1=Tsb[:], op=ALU.mult), ["XPb", "Tsb"], ["XPb"])
                Vv(lambda e: e.tensor_tensor(out=XPb[:, 1], in0=XPb[:, 1], in1=Tcb[:], op=ALU.mult), ["XPb", "Tcb"], ["XPb"])
                Vv(lambda e: e.tensor_tensor(out=XH[:, 0], in0=XH[:, 0], in1=ftmp[:], op=ALU.subtract), ["XH", "ftmp"], ["XH"])
                Vv(lambda e: e.tensor_tensor(out=XH[:, 1], in0=XPb[:, 0], in1=XPb[:, 1], op=ALU.add), ["XPb", "XH"], ["XH"])
            def s5_y(t, c):
                cc_ = slice(128 * t + 64 * c, 128 * t + 64 * c + 64)
                if sw == 2:
                    for a in range(4):
                        oc = slice(128 * a + 64 * c, 128 * a + 64 * c + 64)
                        mm(PS[5][:, oc], Ddiag[a][:], uT_[a][:, cc_], True, False, ["Ddiag%d" % a, uk[a]], ["ps5"])
                        for q in range(4):
                            for part in range(2):
                                if q < 3:
                                    mm(PS[5][32 * q:32 * q + 32, oc], Cst[a][part][:, 32 * q:32 * q + 32], XH[:, part, 4 * a + q, :],
                                       False, False, ["Cst", "XH"], ["ps5"])
                                else:
                                    mm(PS[5][64:128, oc], Cst3[a][part][:, 0:64], XH[:, part, 4 * a + q, :],
                                       False, False, ["Cst", "XH"], ["ps5"])
                        mm(PS[5][:, oc], YBT[:, t % 2, 128 * a:128 * a + 128], Jb[:, 64 * c:64 * c + 64], False, True, ["YBT%d" % (t % 2), "Jb"], ["ps5"])
                else:
                    for a in range(4):
                        for q in range(4):
                            for part in range(2):
                                mm(PS[5][64 * c:64 * c + 64, 128 * a + 32 * q:128 * a + 32 * q + 32], XH[:, part, 4 * a + q, :],
                                   Cst[a][part][:, 32 * q:32 * q + 32], part == 0, part == 1, ["Cst", "XH"], ["ps5"])

            def s5_post(t):
                cols = slice(128 * t, 128 * t + 128)
                if sw == 1:
                    yb = OST[1]
                    A(lambda e, yb=yb: e.activation(out=yb[:], in_=PS[5][:, 0:512], func=AF.Copy), ["ps5"], ["OST1"])
                    DMA(ybscr[blk * 4 + t], yb[:], ["OST1"], ["ybscr%d" % (blk * 4 + t)], d_yb)
                else:
                    t1 = TT[1]
                    A(lambda e: e.activation(out=t1[:], in_=PS[5][:, 0:512], func=AF.Square), ["ps5"], ["tb"])
                    Vv(lambda e: e.tensor_scalar(out=t1[:], in0=t1[:], scalar1=0.044715, scalar2=1.0, op0=ALU.mult, op1=ALU.add), ["tb"], ["tb"])
                    Vv(lambda e: e.tensor_tensor(out=t1[:], in0=t1[:], in1=PS[5][:, 0:512], op=ALU.mult), ["tb", "ps5"], ["tb"])
                    A(lambda e: e.activation(out=t1[:], in_=t1[:], func=AF.Sigmoid, scale=1.5957691216057308), ["tb"], ["tb"])
                    Vv(lambda e: e.tensor_tensor(out=Y2b[:], in0=t1[:], in1=PS[5][:, 0:512], op=ALU.mult), ["tb", "ps5"], ["Y2b"])
                    for ao in range(4):
                        for ai in range(4):
                            mm(PS[6][:, 128 * ao:128 * ao + 128], WG[:, ai, 128 * ao:128 * ao + 128], Y2b[:, 128 * ai:128 * ai + 128],
                               ai == 0, ai == 3, ["WG", "Y2b"], ["ps6"])
                    for ao in range(4):
                        A(lambda e, ao=ao: e.activation(out=sg2[:, 128 * ao:128 * ao + 128], in_=PS[6][:, 128 * ao:128 * ao + 128],
                                                        func=AF.Sigmoid, bias=bgluT[:, ao:ao + 1]), ["ps6", "bgluT"], ["osq"])
                    Vv(lambda e: e.tensor_tensor(out=sg2[:], in0=sg2[:], in1=Y2b[:], op=ALU.mult), ["osq", "Y2b"], ["osq"])
                    for a in range(4):
                        Vv(lambda e, a=a, cols=cols: e.tensor_tensor(out=OB[a][:, cols], in0=sg2[:, 128 * a:128 * a + 128], in1=GBs[a][:, cols], op=ALU.mult),
                           ["osq", "GBs%d" % a], ["OB%d" % a])

            pr = {"g": prep_rest, "pending": []}

            def prep_done():
                return pr["g"] is None

            def pull_prep(n):
                for _ in range(n):
                    if pr["g"] is None:
                        return
                    try:
                        next(pr["g"])
                    except StopIteration:
                        pr["g"] = None

            def finish_prep():
                while pr["g"] is not None:
                    pull_prep(1)

            def want_hgrn(t):
                pr["pending"].append(t)

            def flush_hgrn(maxn):
                n = 0
                while pr["pending"] and prep_done() and n < maxn:
                    hgrn_tile(pr["pending"].pop(0))
                    n += 1

            want_hgrn(0)
            flush_hgrn(1)
            if eblvl >= 5:
                load_ybt(0)
                load_ybt(1)
                s5_A1(0, 0)
                s5_A1(0, 1)
                s5_mod(0, 0)
            for gc in range(8):
                t, c = divmod(gc, 2)
                if gc + 2 < 8:
                    t2, c2 = divmod(gc + 2, 2)
                    if c2 == 0:
                        want_hgrn(t2)
                        flush_hgrn(1)
                    if eblvl >= 5:
                        s5_A1(t2, c2)
                if eblvl >= 5:
                    s5_chain(t, c)
                    if gc + 1 < 8:
                        s5_mod(*divmod(gc + 1, 2))
                    if gc >= 1:
                        tp_, cp_ = divmod(gc - 1, 2)
                        s5_y(tp_, cp_)
                        if cp_ == 1:
                            finish_prep()
                            s5_post(tp_)
                            if tp_ + 2 < 4:
                                load_ybt(tp_ + 2)
                    s5_demod(t, c)
                    if gc == 7:
                        s5_y(t, c)
                        finish_prep()
                        s5_post(t)
                pull_prep(ppull)
                flush_hgrn(2)
                if gen is not None and prep_done():
                    for _ in range(npull):
                        tok = next(gen, "DONE")
                        if tok in ("XNFREE", "DONE") and on_token is not None:
                            on_token(tok)
            finish_prep()
            flush_hgrn(8)

        def even_block(blk, d, sw, bs):
            for _ in even_prep(blk, d, sw, bs):
                pass
            even_tiles(blk, d, sw, bs)

        def run_until_ready(g):
            for tok in g:
                if tok == "READY":
                    return

        def out_proj_gen(ubase, srcs, srckeys):
            def mms(n):
                slot = load_unit(ubase + n)
                bank = n % 4
                for t in range(4):
                    for k in range(8):
                        sk = srckeys[k] if isinstance(srckeys[k], list) else [srckeys[k]]
                        mm(PS[bank][:, 128 * t:128 * t + 128], srcs[k][:, 128 * t:128 * t + 128], WR[slot][:, k, :], k == 0, k == 7,
                           ["wr%d" % slot] + sk, ["ps%d" % bank])

            def add(n):
                bank = n % 4
                Vv(lambda e, n=n, bank=bank: e.tensor_tensor(out=X[:, :, 128 * n:128 * n + 128],
                                                             in0=PS[bank][:, 0:512].rearrange("p (t c) -> p t c", c=128),
                                                             in1=X[:, :, 128 * n:128 * n + 128], op=ALU.add), ["ps%d" % bank, "X"], ["X"])

            for n in range(0, 8, 2):
                mms(n)
                mms(n + 1)
                add(n)
                add(n + 1)
                yield

        def out_proj(ubase, srcs, srckeys):
            for _ in out_proj_gen(ubase, srcs, srckeys):
                pass

        XNv = XN[:].rearrange("p t d -> p (t d)").rearrange("p (f c) -> p f c", c=512)

        def reload_x(blk):
            DMA(X[:], xs[512 * blk:512 * blk + 512, :].rearrange("(t p) d -> p t d", p=128), [], ["X"] + XTk, d_x)

        def odd_gen(blk):
            vec = 0 if blk < 8 else 1
            yield from front_gen(blk, 1, vec, False, False)
            L = 64 if blk < 8 else 256
            cv, r_ = TT[2], TT[3]
            stages = []
            for f in range(8):
                stages += [(U_ODIN + 4 * f + 1, "cg", f), (U_ODIN + 4 * f + 2, "v", f), (U_ODIN + 4 * f + 3, "g", f), (U_ODIN + 4 * f + 0, "bg", f)]
            SK = 2

            def consume(s):
                u, kind, f = stages[s]
                b = s % 4
                y1 = XNv[:, f, :]
                if kind == "cg":
                    A(lambda e, b=b: e.activation(out=cv[:], in_=PS[b][:, 0:512], func=AF.Copy), ["ps%d" % b], ["tc"])
                elif kind == "v":
                    Vv(lambda e, b=b: e.tensor_tensor(out=cv[:], in0=cv[:], in1=PS[b][:, 0:512], op=ALU.mult), ["tc", "ps%d" % b], ["tc"])
                    Vv(lambda e, f=f: e.tensor_scalar(out=r_[:], in0=cv[:], scalar1=cwT[:, 1, f:f + 1], scalar2=cbT[:, f:f + 1],
                                                      op0=ALU.mult, op1=ALU.add), ["tc", "cwT", "cbT"], ["td"])
                    cv3 = cv[:].rearrange("p (r j) -> p r j", j=L)
                    r3 = r_[:].rearrange("p (r j) -> p r j", j=L)
                    Vv(lambda e, f=f, cv3=cv3, r3=r3: e.scalar_tensor_tensor(out=r3[:, :, 1:L], in0=cv3[:, :, 0:L - 1], scalar=cwT[:, 0, f:f + 1],
                                                                            in1=r3[:, :, 1:L], op0=ALU.mult, op1=ALU.add),
                       ["tc", "td", "cwT"], ["td"])
                    Vv(lambda e, f=f, cv3=cv3, r3=r3: e.scalar_tensor_tensor(out=r3[:, :, 0:L - 1], in0=cv3[:, :, 1:L], scalar=cwT[:, 2, f:f + 1],
                                                                            in1=r3[:, :, 0:L - 1], op0=ALU.mult, op1=ALU.add),
                       ["tc", "td", "cwT"], ["td"])
                elif kind == "g":
                    A(lambda e, b=b, y1=y1: e.activation(out=y1, in_=PS[b][:, 0:512], func=AF.Silu), ["ps%d" % b], XK)
                else:
                    Vv(lambda e, b=b: e.tensor_tensor(out=r_[:], in0=r_[:], in1=PS[b][:, 0:512], op=ALU.mult), ["td", "ps%d" % b], ["td"])
                    Vv(lambda e, y1=y1: e.tensor_tensor(out=y1, in0=r_[:], in1=y1, op=ALU.mult), ["td"] + XK, XK)

            for f in range(8):
                base = 4 * f
                proj_fm(stages[base + 0][0], 0)
                proj_fm(stages[base + 1][0], 1)
                proj_fm(stages[base + 2][0], 2)
                consume(base + 0)
                proj_fm(stages[base + 3][0], 3)
                consume(base + 1)
                consume(base + 2)
                consume(base + 3)
                yield
            for _ in out_proj_gen(U_ODOUT + 8 * vec, [XNv[:, f, :] for f in range(8)], [XK] * 8):
                yield
            yield "XNFREE"
            for t in range(4):
                for hf in range(2):
                    A(lambda e, t=t, hf=hf: e.activation(out=OST[hf][:], in_=X[:, t, 512 * hf:512 * hf + 512], func=AF.Square,
                                                         accum_out=ss8[:, 2 * t + hf:2 * t + hf + 1]),
                      ["X"], ["OST%d" % hf, "ss8"])
            Vv(lambda e: e.tensor_tensor(out=ss[:], in0=ss8[:].rearrange("p (t h) -> p t h", h=2)[:, :, 0],
                                         in1=ss8[:].rearrange("p (t h) -> p t h", h=2)[:, :, 1], op=ALU.add), ["ss8"], ["ss"])
            A(lambda e: e.activation(out=rstd[:], in_=ss[:], func=AF.Ln, scale=1.0 / 1024, bias=EPS), ["ss"], ["rstd"])
            A(lambda e: e.activation(out=rstd[:], in_=rstd[:], func=AF.Exp, scale=-0.5), ["rstd"], ["rstd"])
            yield
            for t in range(4):
                Vv(lambda e, t=t: e.scalar_tensor_tensor(out=X[:, t, :], in0=X[:, t, :], scalar=rstd[:, t:t + 1], in1=FG[:],
                                                         op0=ALU.mult, op1=ALU.mult), ["X", "rstd", "FG"], ["X"])
                yield
            DMA(ys[512 * blk:512 * blk + 512, :].rearrange("(t p) d -> p t d", p=128), X[:], ["X"], [], d_out)

        XT = [X[:, i // 2, 512 * (i % 2):512 * (i % 2) + 512] for i in range(5)]
        XTk = ["XB%d" % i for i in range(5)]
        BS0 = dict(V=V, uT=uT, Qt=Qt, Kt=Kt, EB=EB, kV="V", uk=["uT%d" % a for a in range(4)], qk=["Qt%d" % h for h in range(4)],
                   kk=["Kt%d" % h for h in range(4)], kEB="EB", T2=TT, T2k=["ta", "tb", "tc", "td", "te"])
        BS1 = dict(V=GAall, uT=GBs, Qt=OA, Kt=OB, EB=EB2, kV="GAall", uk=["GBs%d" % a for a in range(4)], qk=["OA%d" % h for h in range(4)],
                   kk=["OB%d" % h for h in range(4)], kEB="EB2", T2=XT, T2k=XTk)
        BS0s1 = dict(BS0, T2=XT, T2k=XTk)
        for q in range(4):
            Vv(lambda e, q=q: e.memset(M4[q][:], 0.0), [], ["M4_%d" % q])
        for ci in range(2):
            Vv(lambda e, ci=ci: e.memset(Cpads[ci][:], 0.0), [], ["Cpad%d_%d" % (ci, g8) for g8 in range(8)])
        for a in range(4):
            for p in range(2):
                Vv(lambda e, a=a, p=p: e.memset(Cst3[a][p][:], 0.0), [], ["Cst"])

        if stage >= 1:
            s5_consts(1)
        cast_all()
        blks1 = list(range(NBLK - 1, NBLK - 1 - (nblk1 if stage >= 2 else 0), -1))
        if blks1:
            sets = [BS0s1, BS1]
            A(lambda e: e.activation(out=X[0:1, 0, 0:2], in_=CF[0:1, 0:2], func=AF.Copy), ["CF"], ["X"] + XTk)
            for _ in even_prep(blks1[0], 1, 1, sets[0]):
                pass
            for i, blk in enumerate(blks1):
                gen = even_prep(blks1[i + 1], 1, 1, sets[(i + 1) % 2]) if i + 1 < len(blks1) else None
                even_tiles(blk, 1, 1, sets[i % 2], gen, npull=3)
                if gen is not None:
                    for _ in gen:
                        pass
        if stage >= 3:
            s5_consts(0)
        og = None
        nb2 = nblk2 if stage >= 4 else 0
        pg = None
        if nb2 > 0:
            pg = even_prep(0, 0, 2, BS0)
            run_until_ready(pg)
        for blk in range(nb2):
            st = {"front": False, "reload": False, "pg": None}

            def do_front(blk=blk, st=st):
                if not st["front"]:
                    st["front"] = True
                    if blk + 1 < nb2:
                        st["pg"] = even_prep(blk + 1, 0, 2, BS0)
                        for tok_ in st["pg"]:
                            if tok_ == "FRONT":
                                break

            def do_reload(blk=blk, st=st):
                if not st["reload"]:
                    st["reload"] = True
                    reload_x(blk)

            def on_token(tok, do_front=do_front, do_reload=do_reload):
                if tok == "XNFREE":
                    do_front()
                elif tok == "DONE":
                    do_reload()

            even_tiles(blk, 0, 2, BS0, og, npull=5, prep_rest=pg, ppull=7, on_token=on_token)
            if og is not None:
                for tok in og:
                    if tok == "XNFREE":
                        do_front()
            vec = 0 if blk < 8 else 1
            do_reload()
            do_front()
            pg = st["pg"]
            if dbg and blk == 8:
                for i_ in range(4):
                    DMA(dbg_bf[i_], OA[i_], ["OA%d" % i_], [], d_so)
                    DMA(dbg_bf[4 + i_], OB[i_], ["OB%d" % i_], [], d_so)
            out_proj(U_EVOUT + 8 * vec, [OA[h] for h in range(4)] + [OB[a] for a in range(4)],
                     ["OA%d" % h for h in range(4)] + ["OB%d" % a for a in range(4)])
            if dbg and blk == 8:
                DMA(dbg_x, X[:], ["X"], [], d_so)
            if pg is not None:
                run_until_ready(pg)
            og = odd_gen(blk)
        if og is not None:
            for _ in og:
                pass

        S.emit()
    return nc


_NC_CACHE = {}


def kernel(x_prompt, x_sample, state_hgrn, state_s5_re, state_s5_im, c, c_ctx, norm_g, w_mod, b_mod,
           w_in_even, w_out_even, lb_logits, hgrn_norm_g, s5_lam_re, s5_lam_im, s5_log_dt,
           s5_b_re, s5_b_im, s5_c_re, s5_c_im, s5_d, w_glu, b_glu, w_in_odd, w_out_odd,
           conv_w, conv_b, final_norm_g):
    f = lambda a: np.ascontiguousarray(np.asarray(a, dtype=np.float32))
    if "nc" not in _NC_CACHE:
        _NC_CACHE["nc"] = build_nc()
    nc = _NC_CACHE["nc"]
    cst = _consts()
    shared = {
        "norm_g": f(norm_g), "w_mod": f(w_mod), "b_mod": f(b_mod), "w_in_even": f(w_in_even[0]), "w_out_even": f(w_out_even[0]),
        "lb_logits": f(lb_logits), "hgrn_norm_g": f(hgrn_norm_g[0]), "s5_lam_re": f(s5_lam_re[0]), "s5_lam_im": f(s5_lam_im[0]),
        "s5_log_dt": f(s5_log_dt[0]), "s5_b_re": f(s5_b_re[0]), "s5_b_im": f(s5_b_im[0]), "s5_c_re": f(s5_c_re[0]),
        "s5_c_im": f(s5_c_im[0]), "s5_d": f(s5_d[0]), "w_glu": f(w_glu[0]), "b_glu": f(b_glu[0]), "w_in_odd": f(w_in_odd[0]),
        "w_out_odd": f(w_out_odd[0]), "conv_w": f(conv_w[0]), "conv_b": f(conv_b[0]), "final_norm_g": f(final_norm_g),
        "consts": cst,
    }
    xp = f(x_prompt)
    xsm = f(x_sample)
    in_maps = []
    for i in range(8):
        m = dict(shared)
        m["xs"] = np.concatenate([xsm[i], xp[4 * i:4 * i + 4].reshape(1024, 1024)], axis=0)
        m["cvec"] = np.stack([f(c)[i], f(c_ctx)], axis=0)
        m["st_h"] = f(state_hgrn)[i, 0]
        m["st_re"] = f(state_s5_re)[i, 0]
        m["st_im"] = f(state_s5_im)[i, 0]
        in_maps.append(m)
    res = run_bass_kernel_spmd(nc, in_maps, core_ids=list(range(8)))
    y_prompt = np.zeros((32, 256, 1024), np.float32)
    y_sample = np.zeros((8, 4096, 1024), np.float32)
    nh = np.zeros((32, 1, 2, 4, 128, 128), np.float32)
    nre = np.zeros((32, 1, 2, 32, 64), np.float32)
    nim = np.zeros((32, 1, 2, 32, 64), np.float32)
    for i in range(8):
        r = res.results[i]
        y_sample[i] = r["ys"][:4096]
        y_prompt[4 * i:4 * i + 4] = r["ys"][4096:].reshape(4, 256, 1024)
        nh[4 * i:4 * i + 4, 0] = r["nh"]
        nre[4 * i:4 * i + 4, 0] = r["nre"]
        nim[4 * i:4 * i + 4, 0] = r["nim"]
    return (y_prompt, y_sample, nh, nre, nim)
```

```python
import numpy as np
from contextlib import ExitStack
import concourse.bass as bass
import concourse.mybir as mybir
from concourse.bass_utils import run_bass_kernel_spmd

F32 = mybir.dt.float32
BF16 = mybir.dt.bfloat16
AF = mybir.ActivationFunctionType
ALU = mybir.AluOpType

NBLK = 10
EPS = 1e-6
NSLOT = 5


class DSem:
    def __init__(self, sem, exclusive=True):
        self.sem = sem
        self.count = 0
        self.exclusive = exclusive


class DPool:
    def __init__(self, ds):
        self.ds = ds
        self.i = 0

    def next(self):
        d = self.ds[self.i % len(self.ds)]
        self.i += 1
        return d


class Sched:
    def __init__(self, nc, es):
        self.nc = nc
        self.es = es
        self.engs = {}
        self.sems = []
        self.dsems = []
        for n in ["pe", "act", "dve", "pool", "sp"]:
            s = self.newsem("s_" + n)
            self.engs[n] = dict(ops=[], count=0, sem=s, known={})
        self.lastw = {}
        self.readers = {}

    def newsem(self, name):
        s = self.es.enter_context(self.nc.semaphore(name))
        self.sems.append(s)
        return len(self.sems) - 1

    def dsem(self, name, exclusive=True):
        d = DSem(self.newsem(name), exclusive)
        self.dsems.append(d)
        return d

    def dpool(self, name, n):
        return DPool([self.dsem("%s%d" % (name, i)) for i in range(n)])

    def op(self, eng, fn, reads=(), writes=(), dsem=None):
        if isinstance(dsem, DPool):
            dsem = dsem.next()
        E = self.engs[eng]
        need = {}
        for k in reads:
            t = self.lastw.get(k)
            if t is not None:
                need[t[0]] = max(need.get(t[0], 0), t[1])
        for k in writes:
            t = self.lastw.get(k)
            if t is not None:
                need[t[0]] = max(need.get(t[0], 0), t[1])
            for s, v in self.readers.get(k, {}).items():
                need[s] = max(need.get(s, 0), v)
        if dsem is not None and dsem.exclusive and dsem.count > 0:
            need[dsem.sem] = max(need.get(dsem.sem, 0), dsem.count)
        waits = []
        for s, v in need.items():
            if eng == "pe" and s == E["sem"]:
                continue
            if E["known"].get(s, 0) >= v:
                continue
            E["known"][s] = v
            waits.append((s, v))
        if dsem is None:
            E["count"] += 1
            tok = (E["sem"], E["count"])
            inc = (E["sem"], 1)
        else:
            dsem.count += 16
            tok = (dsem.sem, dsem.count)
            inc = (dsem.sem, 16)
        E["ops"].append((waits, fn, inc))
        for k in reads:
            r = self.readers.setdefault(k, {})
            r[tok[0]] = max(r.get(tok[0], 0), tok[1])
        for k in writes:
            self.lastw[k] = tok
            self.readers[k] = {}
        return tok

    def emit(self):
        nc = self.nc
        sems = self.sems
        S = self

        def run(eng, name):
            for waits, fn, inc in S.engs[name]["ops"]:
                for s, v in waits:
                    eng.wait_ge(sems[s], v)
                ins = fn(eng)
                ins.then_inc(sems[inc[0]], inc[1])

        with nc.Block() as block:
            @block.tensor
            def _(eng):
                run(eng, "pe")

            @block.scalar
            def _(eng):
                run(eng, "act")

            @block.vector
            def _(eng):
                run(eng, "dve")

            @block.gpsimd
            def _(eng):
                run(eng, "pool")

            @block.sync
            def _(eng):
                run(eng, "sp")
                for d in S.dsems:
                    if d.count > 0:
                        eng.wait_ge(sems[d.sem], d.count)


def _consts():
    c = np.zeros((128, 1024), np.float32)
    c[:, 0:128] = np.eye(128, dtype=np.float32)
    c[:, 128:256] = np.eye(128, dtype=np.float32)[::-1]
    s = np.arange(128)[:, None]
    t = np.arange(128)[None, :]
    c[:, 256:384] = ((s // 64 == t // 64) & (s <= t)).astype(np.float32)
    c[:, 384] = (np.arange(128) < 64)
    c[:, 385] = (np.arange(128) >= 64)
    rm = np.ones(512, np.float32)
    rm[::64] = 0.0
    c[:, 386:898] = rm[None, :]
    return c


def build_nc(stage=9, nblk1=NBLK, nblk2=NBLK, setup=9, s5lvl=9, eblvl=9, dbg=False):
    nc = bass.Bass("TRN2", target_bir_lowering=False)
    D = 1024

    def din(name, shape):
        return nc.dram_tensor(name, list(shape), F32, kind="ExternalInput").ap()

    def dout(name, shape):
        return nc.dram_tensor(name, list(shape), F32, kind="ExternalOutput").ap()

    xs = din("xs", [5120, D])
    cvec = din("cvec", [2, D])
    st_h = din("st_h", [2, 4, 128, 128])
    st_re = din("st_re", [2, 32, 64])
    st_im = din("st_im", [2, 32, 64])
    norm_g = din("norm_g", [2, D])
    w_mod = din("w_mod", [2, D, 3 * D])
    b_mod = din("b_mod", [2, 3 * D])
    w_in_even = din("w_in_even", [D, 3584])
    w_out_even = din("w_out_even", [D, D])
    lb_logits = din("lb_logits", [2, 512])
    hgrn_norm_g = din("hgrn_norm_g", [512])
    s5_lam_re = din("s5_lam_re", [2, 32, 64])
    s5_lam_im = din("s5_lam_im", [2, 32, 64])
    s5_log_dt = din("s5_log_dt", [2, 32])
    s5_b_re = din("s5_b_re", [2, 32, 64, 16])
    s5_b_im = din("s5_b_im", [2, 32, 64, 16])
    s5_c_re = din("s5_c_re", [2, 32, 16, 64])
    s5_c_im = din("s5_c_im", [2, 32, 16, 64])
    s5_d = din("s5_d", [512])
    w_glu = din("w_glu", [512, 512])
    b_glu = din("b_glu", [512])
    w_in_odd = din("w_in_odd", [D, 4096])
    w_out_odd = din("w_out_odd", [D, D])
    conv_w = din("conv_w", [3, D])
    conv_b = din("conv_b", [D])
    final_norm_g = din("final_norm_g", [D])
    consts = din("consts", [128, 1024])

    ys = dout("ys", [5120, D])
    nh = dout("nh", [4, 2, 4, 128, 128])
    nre = dout("nre", [4, 2, 32, 64])
    nim = dout("nim", [4, 2, 32, 64])

    NUNIT = 92
    if dbg:
        dbg_bf = nc.dram_tensor('dbg_bf', [8, 128, 512], BF16, kind='ExternalOutput').ap()
        dbg_x = nc.dram_tensor('dbg_x', [128, 4, 1024], F32, kind='ExternalOutput').ap()
    wscr = nc.dram_tensor("wscr", [NUNIT, 128, 1024], BF16, kind="Internal").ap()
    obscr = nc.dram_tensor("obscr", [NBLK * 4, 128, 512], BF16, kind="Internal").ap()
    ybscr = nc.dram_tensor("ybscr", [NBLK * 4, 128, 512], BF16, kind="Internal").ap()

    U_EVOUT = 28
    U_ODIN = 44
    U_ODOUT = 76

    es = ExitStack()
    with es:
        S = Sched(nc, es)

        def sb(name, shape, dt=F32):
            return es.enter_context(nc.sbuf_tensor(name, list(shape), dt))

        CF = sb("CF", [128, 1024])
        identb = sb("identb", [128, 128], BF16)
        Jb = sb("Jb", [128, 128], BF16)
        maskf4 = sb("maskf4", [128, 512], BF16)
        ones128 = sb("ones128", [128, 128], BF16)
        ones_row = sb("ones_row", [1, 128])
        X = sb("X", [128, 4, 1024])
        XN = sb("XN", [128, 4, 1024], BF16)
        hT = sb("hT", [128, 8, 512], BF16)
        WR = [sb("WR%d" % i, [128, 8, 128], BF16) for i in range(NSLOT)]
        WG = sb("WG", [128, 4, 512], BF16)
        FG = sb("FG", [128, 1024])
        ss = sb("ss", [128, 4])
        ss8 = sb("ss8", [128, 8])
        rstd = sb("rstd", [128, 4])
        cT = sb("cT", [128, 8, 2])
        csT = sb("csT", [128, 8, 2])
        bmodT = sb("bmodT", [128, 2, 16])
        bmod_row = sb("bmod_row", [1, 2, 1024])
        normgT = sb("normgT", [128, 2, 8])
        MODT = sb("MODT", [128, 2, 16, 2])
        AT = sb("AT", [128, 2, 8, 2])
        grow = sb("grow", [1, 1024])
        lbl = sb("lbl", [128, 2, 4])
        lb = sb("lb", [128, 4])
        oml = sb("oml", [128, 4])
        noml = sb("noml", [128, 4])
        hg = sb("hg", [128, 4])
        c16 = [sb("c16_%d" % i, [128, 16]) for i in range(16)]
        Bst = [sb("Bst%d" % i, [128, 16, 16]) for i in range(2)]
        Bbar = [sb("Bbar%d" % i, [128, 16, 16]) for i in range(2)]
        Btmp = sb("Btmp", [128, 16, 16])
        M4 = [sb("M4_%d" % q, [128, 128], BF16) for q in range(4)]
        Cpads = [sb("Cpad%d" % i, [128, 128]) for i in range(2)]
        Cpb = sb("Cpb", [128, 128], BF16)
        BTt = [[sb("BTt%d%d" % (a, p), [128, 128], BF16) for p in range(2)] for a in range(4)]
        Cst = [[sb("Cst%d%d" % (a, p), [128, 128], BF16) for p in range(2)] for a in range(4)]
        BTt3 = [[sb("BTt3%d%d" % (a, p), [128, 128], BF16) for p in range(2)] for a in range(4)]
        Cst3 = [[sb("Cst3%d%d" % (a, p), [128, 64], BF16) for p in range(2)] for a in range(4)]
        Tc = sb("Tc", [128, 16, 64])
        Ts = sb("Ts", [128, 16, 64])
        Mg = sb("Mg", [128, 16, 64])
        LL1 = sb("LL1", [128, 2, 16])
        LL2 = sb("LL2", [128, 2, 16])
        Z = sb("Z", [128, 3, 16])
        zt = [sb("zt%d" % i, [128, 2, 16]) for i in range(2)]
        dT = sb("dT", [128, 4])
        Ddiag = [sb("Ddiag%d" % a, [128, 128], BF16) for a in range(4)]
        bgluT = sb("bgluT", [128, 4])
        cwT = sb("cwT", [128, 3, 8])
        cbT = sb("cbT", [128, 8])
        TT = [sb("TT%d" % i, [128, 512]) for i in range(5)]
        Qtall = sb("Qtall", [128, 4, 512], BF16)
        Qt = [Qtall[:, h, :] for h in range(4)]
        Ktall = sb("Ktall", [128, 4, 512], BF16)
        Kt = [Ktall[:, h, :] for h in range(4)]
        V = sb("V", [128, 4, 512], BF16)
        Ktok = sb("Ktok", [128, 4, 512], BF16)
        Abuf = [sb("Abuf%d" % i, [128, 512], BF16) for i in range(2)]
        Sin = [sb("Sin%d" % i, [128, 4, 128], BF16) for i in range(4)]
        S32 = sb("S32", [128, 4, 128])
        EB = sb("EB", [128, 8, 4])
        GAall = sb("GAall", [128, 4, 512], BF16)
        GA = [GAall[:, h, :] for h in range(4)]
        OAall = sb("OAall", [128, 4, 512], BF16)
        OA = [OAall[:, h, :] for h in range(4)]
        OBT = sb("OBT", [128, 2, 512], BF16)
        YBT = sb("YBT", [128, 2, 512], BF16)
        OST = [sb("OST%d" % i, [128, 512], BF16) for i in range(2)]
        osb = TT[4]
        osq = sb("osq", [128, 512], BF16)
        uTall = sb("uTall", [128, 4, 512], BF16)
        uT = [uTall[:, a, :] for a in range(4)]
        GBall = sb("GBall", [128, 4, 512], BF16)
        GBs = [GBall[:, a, :] for a in range(4)]
        OBall = sb("OBall", [128, 4, 512], BF16)
        OB = [OBall[:, a, :] for a in range(4)]
        EB2 = sb("EB2", [128, 8, 4])
        WMM = sb("WMM", [128, 4096])
        W = WMM[:, 0:2048].rearrange("p (r a j) -> p r a j", r=2, a=16)
        MM = WMM[:, 2048:4096].rearrange("p (r a j) -> p r a j", r=2, a=16)
        W1 = sb("W1", [128, 2, 16, 64])
        XP = sb("XP", [128, 2, 16, 64])
        GB = XP[:].rearrange("p r a j -> p (r a j)")[:, 0:1024]
        Ttmp = [XP[:, 0], XP[:, 1]]
        XPb = sb("XPb", [128, 2, 16, 64], BF16)
        XH = sb("XH", [128, 2, 16, 64], BF16)
        Tcb = sb("Tcb", [128, 16, 64], BF16)
        Tsb = sb("Tsb", [128, 16, 64], BF16)
        ftmp = sb("ftmp", [128, 16, 64], BF16)
        LG1 = sb("LG1", [128, 2, 16])
        LG2 = sb("LG2", [128, 2, 16])
        Zo = sb("Zo", [128, 2, 16])
        Y2b = sb("Y2b", [128, 512], BF16)
        sg2 = osq

        PS = [es.enter_context(nc.psum_tensor("PS%d" % i, [128, 512], F32)) for i in range(8)]

        d_c = S.dpool("d_c", 8)
        d_x = S.dsem("d_x")
        d_xf = S.dpool("d_xf", 2)
        d_stg = S.dsem("d_stg")
        d_stg2 = S.dsem("d_stg2")
        d_wst = S.dpool("d_wst", 4)
        d_slot = [S.dsem("d_slot%d" % i) for i in range(NSLOT)]
        d_out = S.dsem("d_out", exclusive=False)
        d_ob = S.dpool("d_ob", 3)
        d_yb = S.dpool("d_yb", 3)
        d_obl = S.dsem("d_obl")
        d_ybl = S.dsem("d_ybl")
        d_st = S.dsem("d_st")
        d_so = S.dpool("d_so", 4)
        POOLQ = [d_obl, d_ybl, d_ob, d_yb, d_out, d_so, d_st, d_wst]

        def A(fn, r=(), w=()):
            return S.op("act", fn, r, w)

        def Vv(fn, r=(), w=()):
            return S.op("dve", fn, r, w)

        def Pl(fn, r=(), w=()):
            return S.op("pool", fn, r, w)

        def PE(fn, r=(), w=()):
            return S.op("pe", fn, r, w)

        def DMA(out, in_, r, w, ds, slow=False, q=None):
            if q is None:
                q = "pool" if any(ds is x for x in POOLQ) else "sp"
            if slow:
                return S.op(q, lambda e: e.dma_start(out=out, in_=in_, allow_slow_non_contiguous=True), r, w, ds)
            return S.op(q, lambda e: e.dma_start(out=out, in_=in_), r, w, ds)

        def mm(out, lhsT, rhs, start, stop, r, w):
            return PE(lambda e: e.matmul(out, lhsT=lhsT, rhs=rhs, start=start, stop=stop), r, w)

        def tt_op(eng, out, in0, in1, op, r, w):
            return S.op(eng, lambda e: e.tensor_tensor(out=out, in0=in0, in1=in1, op=op), r, w)

        def bc(ap, shape, axis):
            return ap.unsqueeze(axis).to_broadcast(list(shape))

        DMA(CF[:], consts, [], ["CF"], d_c)
        Vv(lambda e: e.tensor_copy(out=identb[:], in_=CF[:, 0:128]), ["CF"], ["identb"])
        Vv(lambda e: e.tensor_copy(out=Jb[:], in_=CF[:, 128:256]), ["CF"], ["Jb"])
        for h in range(4):
            Vv(lambda e, h=h: e.tensor_copy(out=maskf4[:, 128 * h:128 * h + 128], in_=CF[:, 256:384]), ["CF"], ["maskf4"])
        Vv(lambda e: e.memset(ones128[:], 1.0 / 128), [], ["ones128"])
        Vv(lambda e: e.memset(ones_row[:], 1.0), [], ["ones_row"])
        identf = CF[:, 0:128]
        halfmask = CF[:, 384:386]
        rmask = CF[:, 386:898]
        DMA(FG[:], final_norm_g.partition_broadcast(128), [], ["FG"], d_c)
        for v in range(2):
            DMA(cT[:, :, v], cvec[v].rearrange("(k p) -> p k", p=128), [], ["cT"], d_c, slow=True)
        for l in range(2):
            DMA(bmodT[:, l, :], b_mod[l, 0:2048].rearrange("(j p) -> p j", p=128), [], ["bmodT"], d_c, slow=True)
        DMA(bmod_row[:], b_mod[:, 2048:3072].rearrange("(o l) c -> o l c", o=1), [], ["bmod_row"], d_c)
        for l in range(2):
            DMA(normgT[:, l, :], norm_g[l].rearrange("(k p) -> p k", p=128), [], ["normgT"], d_c, slow=True)
        for r_ in range(2):
            DMA(lbl[:, r_, :], lb_logits[r_].rearrange("(h p) -> p h", p=128), [], ["lbl"], d_c, slow=True)
        DMA(hg[:], hgrn_norm_g.rearrange("(h p) -> p h", p=128), [], ["hg"], d_c, slow=True)
        DMA(dT[:], s5_d.rearrange("(a p) -> p a", p=128), [], ["dT"], d_c, slow=True)
        DMA(bgluT[:], b_glu.rearrange("(a p) -> p a", p=128), [], ["bgluT"], d_c, slow=True)
        for j_ in range(3):
            DMA(cwT[:, j_, :], conv_w[j_].rearrange("(f p) -> p f", p=128), [], ["cwT"], d_c, slow=True)
        DMA(cbT[:], conv_b.rearrange("(f p) -> p f", p=128), [], ["cbT"], d_c, slow=True)

        A(lambda e: e.activation(out=csT[:], in_=cT[:], func=AF.Silu), ["cT"], ["csT"])
        Vv(lambda e: e.tensor_tensor(out=lb[:], in0=lbl[:, 0, :], in1=lbl[:, 1, :], op=ALU.subtract), ["lbl"], ["lb"])
        A(lambda e: e.activation(out=lb[:], in_=lb[:], func=AF.Sigmoid), ["lb"], ["lb"])
        Vv(lambda e: e.tensor_scalar(out=oml[:], in0=lb[:], scalar1=-1.0, scalar2=1.0, op0=ALU.mult, op1=ALU.add), ["lb"], ["oml"])
        Vv(lambda e: e.tensor_scalar(out=noml[:], in0=oml[:], scalar1=-1.0, scalar2=None, op0=ALU.mult), ["oml"], ["noml"])
        for a in range(4):
            Vv(lambda e, a=a: e.tensor_scalar(out=Ddiag[a][:], in0=identf, scalar1=dT[:, a:a + 1], scalar2=None, op0=ALU.mult),
               ["CF", "dT"], ["Ddiag%d" % a])

        Xs = X[:].rearrange("p t d -> p (t d)").rearrange("p (k c) -> p k c", c=512)
        XNs = XN[:].rearrange("p t d -> p (t d)").rearrange("p (k c) -> p k c", c=512)
        STG = [dict(f=Xs, b=XNs, fk=["X"], bk=["XN", "XNb", "XNc"], ds=d_stg),
               dict(f=WMM[:].rearrange("p (k c) -> p k c", c=512), b=hT[:], fk=["W", "MM"], bk=["hT", "hTb", "hTc"], ds=d_stg2)]
        stg_i = {"i": 0}

        def next_stg():
            s = STG[stg_i["i"] % 2]
            stg_i["i"] += 1
            return s

        for l in range(2 if setup >= 2 else 0):
            for pc in range(4):
                sg = next_stg()
                DMA(sg["f"], w_mod[l][:, 512 * pc:512 * pc + 512].rearrange("(k p) c -> p k c", p=128), [], sg["fk"], sg["ds"])
                for i in range(4):
                    for k in range(8):
                        mm(PS[0][:, 2 * i:2 * i + 2], sg["f"][:, k, 128 * i:128 * i + 128], csT[:, k, :], k == 0, k == 7,
                           sg["fk"] + ["csT"], ["ps0"])
                Vv(lambda e, l=l, pc=pc: e.tensor_tensor(
                    out=MODT[:, l, 4 * pc:4 * pc + 4, :], in0=PS[0][:, 0:8].rearrange("p (j v) -> p j v", v=2),
                    in1=bc(bmodT[:, l, 4 * pc:4 * pc + 4], [128, 4, 2], 2), op=ALU.add), ["ps0", "bmodT"], ["MODT"])
            Vv(lambda e, l=l: e.tensor_scalar(out=AT[:, l], in0=MODT[:, l, 8:16, :], scalar1=1.0, scalar2=None, op0=ALU.add),
               ["MODT"], ["AT"])
            Vv(lambda e, l=l: e.tensor_tensor(out=AT[:, l], in0=AT[:, l], in1=bc(normgT[:, l, :], [128, 8, 2], 2), op=ALU.mult),
               ["AT", "normgT"], ["AT"])

        def gate_bcast(l, v):
            for pc in range(2):
                sg = next_stg()
                DMA(sg["f"], w_mod[l][:, 2048 + 512 * pc:2048 + 512 * pc + 512].rearrange("(k p) c -> p k c", p=128), [], sg["fk"], sg["ds"])
                for k in range(8):
                    mm(PS[1][0:1, 0:512], csT[:, k, v:v + 1], sg["f"][:, k, :], k == 0, k == 7, sg["fk"] + ["csT"], ["ps1"])
                Vv(lambda e, pc=pc: e.tensor_tensor(out=grow[0:1, 512 * pc:512 * pc + 512], in0=PS[1][0:1, 0:512],
                                                    in1=bmod_row[0:1, l, 512 * pc:512 * pc + 512], op=ALU.add),
                   ["ps1", "bmod_row"], ["grow"])
            for pc in range(2):
                mm(PS[2][:, 0:512], ones_row[0:1, 0:128], grow[0:1, 512 * pc:512 * pc + 512], True, True,
                   ["ones_row", "grow"], ["ps2"])
                A(lambda e, pc=pc: e.activation(out=GB[:, 512 * pc:512 * pc + 512], in_=PS[2][:, 0:512], func=AF.Copy),
                  ["ps2"], ["XP"])

        def cast_piece(src_ap, units, gated_cols=None):
            sg = next_stg()
            Fs, Bs, fk, bk = sg["f"], sg["b"], sg["fk"], sg["bk"]
            DMA(Fs, src_ap.rearrange("(k p) c -> p k c", p=128), [], fk, sg["ds"])
            Bf = Bs.rearrange("p k c -> p (k c)")
            Bp = Bf.rearrange("p (u k c) -> p k u c", u=4, k=8)

            def src4(k0, k1):
                return Fs[:, k0:k1, :].rearrange("p k (u c) -> p k u c", u=4)

            if gated_cols is None:
                A(lambda e: e.activation(out=Bp[:, 0:4], in_=src4(0, 4), func=AF.Copy), fk, [bk[0]])
                Vv(lambda e: e.tensor_copy(out=Bp[:, 4:8], in_=src4(4, 8)), fk, [bk[1], bk[2]])
            else:
                g = GB[:, gated_cols:gated_cols + 512].rearrange("p (u c) -> p u c", u=4)
                Vv(lambda e: e.tensor_tensor(out=Bp[:, 0:8], in0=src4(0, 8), in1=bc(g, [128, 8, 4, 128], 1), op=ALU.mult),
                   fk + ["XP"], [bk[0], bk[1], bk[2]])
            for i, u in enumerate(units):
                DMA(wscr[u], Bf[:, 1024 * i:1024 * i + 1024], bk, ["wscr%d" % u], d_wst)

        def cast_all():
            for pc in range(7 if setup >= 3 else 0):
                cast_piece(w_in_even[:, 512 * pc:512 * pc + 512], [4 * pc + i for i in range(4)])
            if setup >= 4:
                DMA(Xs[:, 0:4, :], w_glu.rearrange("(k p) c -> p k c", p=128), [], ["X"], d_stg)
                A(lambda e: e.activation(out=WG[:], in_=Xs[:, 0:4, :], func=AF.Copy), ["X"], ["WG"])
            for l, (wsrc, ubase) in enumerate([(w_out_even, U_EVOUT), (w_out_odd, U_ODOUT)] if setup >= 5 else []):
                for v in range(2):
                    gate_bcast(l, v)
                    for pc in range(2):
                        cast_piece(wsrc[:, 512 * pc:512 * pc + 512], [ubase + 8 * v + 4 * pc + i for i in range(4)], gated_cols=512 * pc)
            for pc in range(8 if setup >= 6 else 0):
                units = []
                for i in range(4):
                    ch = 4 * pc + i
                    j, f = ch // 8, ch % 8
                    units.append(U_ODIN + 4 * f + j)
                cast_piece(w_in_odd[:, 512 * pc:512 * pc + 512], units)


        def s5_consts(d):
            lre, lim, ldt, dt, ar, th, m_, cc, sn, t1, t2, t3, Lr, Li, cre, cim = c16

            def T(fn, r, w):
                return Vv(fn, r, w)

            for gl in range(2):
                DMA(lre[64 * gl:64 * gl + 64, :], s5_lam_re[d].rearrange("(P two) p -> two p P", two=2)[gl], [], ["lre"], d_c, slow=True)
                DMA(lim[64 * gl:64 * gl + 64, :], s5_lam_im[d].rearrange("(P two) p -> two p P", two=2)[gl], [], ["lim"], d_c, slow=True)
                DMA(ldt[64 * gl:64 * gl + 64, :], s5_log_dt[d].rearrange("(P two) -> two P", two=2)[gl].partition_broadcast(64),
                    [], ["ldt"], d_c, slow=True)
                DMA(Bst[0][64 * gl:64 * gl + 64, :, :], s5_b_re[d].rearrange("(P two) p s -> two p P s", two=2)[gl], [], ["Bst0"], d_c)
                DMA(Bst[1][64 * gl:64 * gl + 64, :, :], s5_b_im[d].rearrange("(P two) p s -> two p P s", two=2)[gl], [], ["Bst1"], d_c)
            if s5lvl < 2:
                return
            A(lambda e: e.activation(out=dt[:], in_=ldt[:], func=AF.Exp), ["ldt"], ["dt"])
            T(lambda e: e.tensor_tensor(out=ar[:], in0=lre[:], in1=dt[:], op=ALU.mult), ["lre", "dt"], ["ar"])
            T(lambda e: e.tensor_tensor(out=th[:], in0=lim[:], in1=dt[:], op=ALU.mult), ["lim", "dt"], ["th"])
            A(lambda e: e.activation(out=m_[:], in_=ar[:], func=AF.Exp), ["ar"], ["m_"])
            A(lambda e: e.activation(out=sn[:], in_=th[:], func=AF.Sin, scale=1.0 / 16), ["th"], ["sn"])
            A(lambda e: e.activation(out=cc[:], in_=th[:], func=AF.Sin, scale=1.0 / 16, bias=float(np.pi / 2)), ["th"], ["cc"])

            def csq(c_, s_):
                T(lambda e: e.tensor_tensor(out=t1[:], in0=c_[:], in1=c_[:], op=ALU.mult), ["cc", "sn"], ["t1"])
                T(lambda e: e.tensor_tensor(out=t2[:], in0=s_[:], in1=s_[:], op=ALU.mult), ["cc", "sn"], ["t2"])
                T(lambda e: e.tensor_tensor(out=t3[:], in0=c_[:], in1=s_[:], op=ALU.mult), ["cc", "sn"], ["t3"])
                T(lambda e: e.tensor_tensor(out=c_[:], in0=t1[:], in1=t2[:], op=ALU.subtract), ["t1", "t2"], ["cc"])
                T(lambda e: e.tensor_scalar(out=s_[:], in0=t3[:], scalar1=2.0, scalar2=None, op0=ALU.mult), ["t3"], ["sn"])

            for _ in range(4):
                csq(cc, sn)
            T(lambda e: e.tensor_tensor(out=Lr[:], in0=m_[:], in1=cc[:], op=ALU.mult), ["m_", "cc"], ["Lr"])
            T(lambda e: e.tensor_tensor(out=Li[:], in0=m_[:], in1=sn[:], op=ALU.mult), ["m_", "sn"], ["Li"])
            T(lambda e: e.tensor_tensor(out=t1[:], in0=lre[:], in1=lre[:], op=ALU.mult), ["lre"], ["t1"])
            T(lambda e: e.tensor_tensor(out=t2[:], in0=lim[:], in1=lim[:], op=ALU.mult), ["lim"], ["t2"])
            T(lambda e: e.tensor_tensor(out=t1[:], in0=t1[:], in1=t2[:], op=ALU.add), ["t1", "t2"], ["t1"])
            T(lambda e: e.reciprocal(out=t1[:], in_=t1[:]), ["t1"], ["t1"])
            T(lambda e: e.tensor_scalar(out=t2[:], in0=Lr[:], scalar1=-1.0, scalar2=None, op0=ALU.add), ["Lr"], ["t2"])
            T(lambda e: e.tensor_tensor(out=cre[:], in0=t2[:], in1=lre[:], op=ALU.mult), ["t2", "lre"], ["cre"])
            T(lambda e: e.tensor_tensor(out=t3[:], in0=Li[:], in1=lim[:], op=ALU.mult), ["Li", "lim"], ["t3"])
            T(lambda e: e.tensor_tensor(out=cre[:], in0=cre[:], in1=t3[:], op=ALU.add), ["cre", "t3"], ["cre"])
            T(lambda e: e.tensor_tensor(out=cre[:], in0=cre[:], in1=t1[:], op=ALU.mult), ["cre", "t1"], ["cre"])
            T(lambda e: e.tensor_tensor(out=cim[:], in0=Li[:], in1=lre[:], op=ALU.mult), ["Li", "lre"], ["cim"])
            T(lambda e: e.tensor_tensor(out=t3[:], in0=t2[:], in1=lim[:], op=ALU.mult), ["t2", "lim"], ["t3"])
            T(lambda e: e.tensor_tensor(out=cim[:], in0=cim[:], in1=t3[:], op=ALU.subtract), ["cim", "t3"], ["cim"])
            T(lambda e: e.tensor_tensor(out=cim[:], in0=cim[:], in1=t1[:], op=ALU.mult), ["cim", "t1"], ["cim"])
            if s5lvl < 3:
                return
            crb = bc(cre[:], [128, 16, 16], 2)
            cib = bc(cim[:], [128, 16, 16], 2)
            T(lambda e: e.tensor_tensor(out=Bbar[0][:], in0=Bst[0][:], in1=crb, op=ALU.mult), ["Bst0", "cre"], ["Bbar0"])
            T(lambda e: e.tensor_tensor(out=Btmp[:], in0=Bst[1][:], in1=cib, op=ALU.mult), ["Bst1", "cim"], ["Btmp"])
            T(lambda e: e.tensor_tensor(out=Bbar[0][:], in0=Bbar[0][:], in1=Btmp[:], op=ALU.subtract), ["Bbar0", "Btmp"], ["Bbar0"])
            T(lambda e: e.tensor_tensor(out=Bbar[1][:], in0=Bst[1][:], in1=crb, op=ALU.mult), ["Bst1", "cre"], ["Bbar1"])
            T(lambda e: e.tensor_tensor(out=Btmp[:], in0=Bst[0][:], in1=cib, op=ALU.mult), ["Bst0", "cim"], ["Btmp"])
            T(lambda e: e.tensor_tensor(out=Bbar[1][:], in0=Bbar[1][:], in1=Btmp[:], op=ALU.add), ["Bbar1", "Btmp"], ["Bbar1"])
            if s5lvl < 4:
                return
            hm = bc(halfmask, [128, 2, 16], 2)
            for a in range(4):
                for part in range(2):
                    for q in range(4):
                        P = 4 * a + q
                        T(lambda e, q=q, P=P, part=part: e.tensor_tensor(
                            out=M4[q][:, 32 * q:32 * q + 32].rearrange("p (g s) -> p g s", s=16),
                            in0=bc(Bbar[part][:, P, :], [128, 2, 16], 1), in1=hm, op=ALU.mult),
                          ["Bbar%d" % part, "CF"], ["M4_%d" % q])
                        mm(PS[3][:, 0:128], M4[q][:], identb[:], q == 0, q == 3, ["M4_%d" % q, "identb"], ["ps3"])
                    A(lambda e, a=a, part=part: e.activation(out=BTt[a][part][:], in_=PS[3][:, 0:128], func=AF.Copy),
                      ["ps3"], ["BTt"])
                    A(lambda e, a=a, part=part: e.activation(out=BTt3[a][part][64:128, :], in_=PS[3][64:128, 0:128], func=AF.Copy),
                      ["ps3"], ["BTt"])
                    Vv(lambda e, a=a, part=part: e.memset(BTt3[a][part][64:96, :], 0.0), ["BTt"], ["BTt"])
            if s5lvl < 5:
                return
            for a in range(4):
                for part in range(2):
                    src = (s5_c_re if part == 0 else s5_c_im)
                    ci = (2 * a + part) % 2
                    Cpad = Cpads[ci]
                    ck = ["Cpad%d_%d" % (ci, g8) for g8 in range(8)]
                    for g8 in range(8):
                        DMA(Cpad[16 * g8:16 * g8 + 16, 64 * (g8 % 2):64 * (g8 % 2) + 64], src[d, 8 * a + g8], [], [ck[g8]], d_c)
                    if s5lvl < 5.5:
                        continue
                    Vv(lambda e, Cpad=Cpad: e.tensor_copy(out=Cpb[:], in_=Cpad[:]), ck, ["Cpb"])
                    mm(PS[3][:, 128:256], Cpb[:], identb[:], True, True, ["Cpb", "identb"], ["ps3b"])
                    if s5lvl < 5.7:
                        continue
                    A(lambda e, a=a, part=part: e.activation(out=Cst[a][part][:], in_=PS[3][:, 128:256], func=AF.Copy,
                                                             scale=(1.0 if part == 0 else -1.0)), ["ps3b"], ["Cst"])
                    A(lambda e, a=a, part=part: e.activation(out=Cst3[a][part][:, 32:64], in_=PS[3][:, 224:256], func=AF.Copy,
                                                             scale=(1.0 if part == 0 else -1.0)), ["ps3b"], ["Cst"])
            if s5lvl < 6:
                return
            Vv(lambda e: e.memset(Tc[:, :, 0:1], 1.0), [], ["Tc"])
            Vv(lambda e: e.memset(Ts[:, :, 0:1], 0.0), [], ["Ts"])
            for kk in range(6):
                n = 1 << kk
                ucb = bc(cc[:], [128, 16, n], 2)
                usb = bc(sn[:], [128, 16, n], 2)
                ta_, tb_ = Ttmp[0][:, :, 0:n], Ttmp[1][:, :, 0:n]
                T(lambda e, n=n, ucb=ucb, ta_=ta_: e.tensor_tensor(out=ta_, in0=Tc[:, :, 0:n], in1=ucb, op=ALU.mult), ["Tc", "cc"], ["XP"])
                T(lambda e, n=n, usb=usb, tb_=tb_: e.tensor_tensor(out=tb_, in0=Ts[:, :, 0:n], in1=usb, op=ALU.mult), ["Ts", "sn"], ["XP"])
                T(lambda e, n=n, ta_=ta_, tb_=tb_: e.tensor_tensor(out=Tc[:, :, n:2 * n], in0=ta_, in1=tb_, op=ALU.subtract),
                  ["XP", "XP", "Tc"], ["Tc"])
                T(lambda e, n=n, usb=usb, ta_=ta_: e.tensor_tensor(out=ta_, in0=Tc[:, :, 0:n], in1=usb, op=ALU.mult), ["Tc", "sn"], ["XP"])
                T(lambda e, n=n, ucb=ucb, tb_=tb_: e.tensor_tensor(out=tb_, in0=Ts[:, :, 0:n], in1=ucb, op=ALU.mult), ["Ts", "cc"], ["XP"])
                T(lambda e, n=n, ta_=ta_, tb_=tb_: e.tensor_tensor(out=Ts[:, :, n:2 * n], in0=ta_, in1=tb_, op=ALU.add),
                  ["XP", "XP", "Ts"], ["Ts"])
                if kk < 5:
                    csq(cc, sn)
            Vv(lambda e: e.memset(Mg[:, :, 0:1], 0.0), [], ["Mg"])
            Vv(lambda e: e.tensor_copy(out=Mg[:, :, 1:64], in_=bc(m_[:], [128, 16, 63], 2)), ["m_", "Mg"], ["Mg"])
            Vv(lambda e: e.tensor_copy(out=LL1[:], in_=bc(Lr[:], [128, 2, 16], 1)), ["Lr"], ["LL1"])
            Vv(lambda e: e.tensor_scalar(out=LL2[:, 0, :], in0=Li[:], scalar1=-1.0, scalar2=None, op0=ALU.mult), ["Li"], ["LL2"])
            Vv(lambda e: e.tensor_copy(out=LL2[:, 1, :], in_=Li[:]), ["Li", "LL2"], ["LL2"])
            T(lambda e: e.tensor_tensor(out=t1[:], in0=Lr[:], in1=Tc[:, :, 63], op=ALU.mult), ["Lr", "Tc"], ["t1"])
            T(lambda e: e.tensor_tensor(out=t2[:], in0=Li[:], in1=Ts[:, :, 63], op=ALU.mult), ["Li", "Ts"], ["t2"])
            T(lambda e: e.tensor_tensor(out=t1[:], in0=t1[:], in1=t2[:], op=ALU.subtract), ["t1", "t2"], ["t1"])
            T(lambda e: e.tensor_tensor(out=t2[:], in0=Lr[:], in1=Ts[:, :, 63], op=ALU.mult), ["Lr", "Ts", "t1"], ["t2"])
            T(lambda e: e.tensor_tensor(out=t3[:], in0=Li[:], in1=Tc[:, :, 63], op=ALU.mult), ["Li", "Tc"], ["t3"])
            T(lambda e: e.tensor_tensor(out=t2[:], in0=t2[:], in1=t3[:], op=ALU.add), ["t2", "t3"], ["t2"])
            Vv(lambda e: e.tensor_copy(out=LG1[:], in_=bc(t1[:], [128, 2, 16], 1)), ["t1"], ["LG1"])
            Vv(lambda e: e.tensor_scalar(out=LG2[:, 0, :], in0=t2[:], scalar1=-1.0, scalar2=None, op0=ALU.mult), ["t2"], ["LG2"])
            Vv(lambda e: e.tensor_copy(out=LG2[:, 1, :], in_=t2[:]), ["t2", "LG2"], ["LG2"])
            A(lambda e: e.activation(out=Tcb[:], in_=Tc[:], func=AF.Copy), ["Tc"], ["Tcb"])
            A(lambda e: e.activation(out=Tsb[:], in_=Ts[:], func=AF.Copy), ["Ts"], ["Tsb"])

        ring = {"pos": 0}

        def load_unit(u):
            slot = ring["pos"] % NSLOT
            ring["pos"] += 1
            DMA(WR[slot][:].rearrange("p k c -> p (k c)"), wscr[u], ["wscr%d" % u], ["wr%d" % slot], d_slot[slot],
                q=("pool" if u >= U_ODIN else "sp"))
            return slot

        def proj_fm(u, bank, src="hT"):
            slot = load_unit(u)
            for k in range(8):
                mm(PS[bank][:, 0:512], WR[slot][:, k, :], hT[:, k, :], k == 0, k == 7, ["wr%d" % slot, src], ["ps%d" % bank])

        Xf = hT[:].rearrange("p k c -> p (k c)").bitcast(F32).rearrange("p (t d) -> p t d", t=2)
        HK = ["hT", "hTb", "hTc"]
        XK = ["XN", "XNb", "XNc"]

        def front_gen(blk, l, vec, rev, stream):
            for t in range(4):
                if stream:
                    xin = Xf[:, t % 2, :]
                    xk = HK
                    DMA(xin, xs[512 * blk + 128 * t:512 * blk + 128 * t + 128, :], [], HK, d_xf)
                else:
                    xin = X[:, t, :]
                    xk = ["X"]
                A(lambda e, t=t, xin=xin: e.activation(out=XN[:, t, :], in_=xin, func=AF.Square, accum_out=ss[:, t:t + 1]),
                  xk, XK + ["ss"])
                A(lambda e, t=t: e.activation(out=rstd[:, t:t + 1], in_=ss[:, t:t + 1], func=AF.Ln, scale=1.0 / 1024, bias=EPS),
                  ["ss"], ["rstd"])
                A(lambda e, t=t: e.activation(out=rstd[:, t:t + 1], in_=rstd[:, t:t + 1], func=AF.Exp, scale=-0.5), ["rstd"], ["rstd"])
                A(lambda e, t=t, xin=xin: e.activation(out=XN[:, t, :], in_=xin, func=AF.Copy, scale=rstd[:, t:t + 1]),
                  xk + ["rstd"], XK)
                yield
            perm = Jb if rev else identb
            for k in range(8):
                bank = k % 2
                for t in range(4):
                    pos = 3 - t if rev else t
                    mm(PS[bank][:, 128 * pos:128 * pos + 128], XN[:, t, 128 * k:128 * k + 128], perm[:], True, True,
                       ["XN", "Jb", "identb"], ["ps%d" % bank])
                A(lambda e, k=k, bank=bank: e.activation(out=hT[:, k, :], in_=PS[bank][:, 0:512], func=AF.Identity,
                                                         scale=AT[:, l, k, vec:vec + 1], bias=MODT[:, l, k, vec:vec + 1]),
                  ["ps%d" % bank, "AT", "MODT"], HK)
                if k % 2 == 1:
                    yield

        def front(blk, l, vec, rev, stream):
            for _ in front_gen(blk, l, vec, rev, stream):
                pass

        def even_prep(blk, d, sw, bs):
            rev = (d == 1)
            vec = 0 if blk < 8 else 1
            V_, uT_, Qt_, Kt_, EB_ = bs["V"], bs["uT"], bs["Qt"], bs["Kt"], bs["EB"]
            kV, uk, qk, kk, kEB = bs["kV"], bs["uk"], bs["qk"], bs["kk"], bs["kEB"]
            yield from front_gen(blk, 0, vec, rev, True)
            yield "FRONT"
            for a in range(4):
                proj_fm(20 + a, a % 4)
                A(lambda e, a=a: e.activation(out=uT_[a], in_=PS[a % 4][:, 0:512], func=AF.Copy), ["ps%d" % (a % 4)], [uk[a]])
                yield
            yield "READY"
            if sw == 2:
                for a in range(4):
                    proj_fm(24 + a, a % 4)
                    A(lambda e, a=a: e.activation(out=GBs[a], in_=PS[a % 4][:, 0:512], func=AF.Silu), ["ps%d" % (a % 4)], ["GBs%d" % a])
                    yield
            for i in range(4):
                slot = load_unit(12 + i)
                bank = i % 4
                for t in range(4):
                    for k in range(8):
                        mm(PS[bank][:, 128 * t:128 * t + 128], hT[:, k, 128 * t:128 * t + 128], WR[slot][:, k, :], k == 0, k == 7,
                           ["wr%d" % slot, "hT"], ["ps%d" % bank])
                A(lambda e, i=i, bank=bank: e.activation(out=V_[:, :, 128 * i:128 * i + 128],
                                                         in_=PS[bank][:, 0:512].rearrange("p (t c) -> p t c", c=128), func=AF.Copy),
                  ["ps%d" % bank], [kV])
                yield
            tsets = [(TT, ["ta", "tb", "tc", "td", "te"]), (bs["T2"], bs["T2k"])]
            two_sets = bs["T2"] is not TT

            def st_proj(i, h):
                proj_fm(h, 2 * i)
                proj_fm((8 if d == 1 else 4) + h, 2 * i + 1)

            def st_gate(i, h):
                (ta, tb, tc_, td, te), (ka, kb, kc, kd, ke) = tsets[i]
                A(lambda e, i=i, ta=ta: e.activation(out=ta[:], in_=PS[2 * i + 1][:, 0:512], func=AF.Sigmoid), ["ps%d" % (2 * i + 1)], [ka])
                A(lambda e, h=h, ta=ta, tb=tb: e.activation(out=tb[:], in_=ta[:], func=AF.Ln, scale=oml[:, h:h + 1], bias=lb[:, h:h + 1]),
                  [ka, "oml", "lb"], [kb])
                A(lambda e, h=h, ta=ta: e.activation(out=ta[:], in_=ta[:], func=AF.Identity, scale=noml[:, h:h + 1], bias=oml[:, h:h + 1]),
                  [ka, kb, "oml", "noml"], [ka])

            def st_scan(i, h):
                (ta, tb, tc_, td, te), (ka, kb, kc, kd, ke) = tsets[i]
                Vv(lambda e, tb=tb, tc_=tc_: e.tensor_tensor_scan(out=tc_[:], data0=rmask, data1=tb[:], initial=0.0, op0=ALU.mult, op1=ALU.add),
                   [kb, "CF"], [kc])

            def st_exp(i, h):
                (ta, tb, tc_, td, te), (ka, kb, kc, kd, ke) = tsets[i]
                A(lambda e, tc_=tc_, td=td: e.activation(out=td[:], in_=tc_[:], func=AF.Exp), [kc], [kd])
                A(lambda e, tc_=tc_, te=te: e.activation(out=te[:], in_=tc_[:], func=AF.Exp, scale=-1.0), [kc], [ke])
                A(lambda e, h=h, tc_=tc_: e.activation(out=EB_[:, :, h], in_=tc_[:].rearrange("p (c j) -> p c j", j=64)[:, :, 63], func=AF.Exp),
                  [kc], [kEB])

            def st_qk(i, h):
                (ta, tb, tc_, td, te), (ka, kb, kc, kd, ke) = tsets[i]
                Vv(lambda e, h=h, i=i, td=td: e.tensor_tensor(out=Qt_[h], in0=PS[2 * i][:, 0:512], in1=td[:], op=ALU.mult),
                   ["ps%d" % (2 * i), kd], [qk[h]])
                Vv(lambda e, h=h, ta=ta, te=te: e.tensor_tensor(out=Kt_[h], in0=ta[:], in1=te[:], op=ALU.mult), [ka, ke], [kk[h]])

            for hp in range(2):
                hs_ = [(0, 2 * hp), (1, 2 * hp + 1)]
                for i, h in hs_:
                    st_proj(i, h)
                if two_sets:
                    for st in (st_gate, st_scan, st_exp, st_qk):
                        for i, h in hs_:
                            st(i, h)
                else:
                    for i, h in hs_:
                        for st in (st_gate, st_scan, st_exp, st_qk):
                            st(i, h)
                yield
            if sw == 2:
                for h in range(4):
                    proj_fm(16 + h, h % 4)
                    A(lambda e, h=h: e.activation(out=GA[h], in_=PS[h % 4][:, 0:512], func=AF.Silu), ["ps%d" % (h % 4)], ["GA%d" % h])
                    yield

        def even_tiles(blk, d, sw, bs, gen=None, npull=3, prep_rest=None, ppull=4, on_token=None):
            rev = (d == 1)
            V_, uT_, Qt_, Kt_, EB_ = bs["V"], bs["uT"], bs["Qt"], bs["Kt"], bs["EB"]
            kV, uk, qk, kk, kEB = bs["kV"], bs["uk"], bs["qk"], bs["kk"], bs["kEB"]
            first_blk = 0 if not rev else 7

            def is_start(t):
                return (blk == first_blk and t == 0) or (blk >= 8 and t % 2 == 0)

            def is_end(t):
                return blk >= 8 and t % 2 == 1

            def seq_of(t):
                at = t if not rev else 3 - t
                return 2 * (blk - 8) + (at // 2)

            def hgrn_tile(t):
                cols = slice(128 * t, 128 * t + 128)
                if sw == 2:
                    DMA(OBT[:, t % 2, :], obscr[blk * 4 + (3 - t)], ["obscr%d" % (blk * 4 + 3 - t)], ["OBT%d" % (t % 2)], d_obl)
                if is_start(t):
                    if blk >= 8:
                        Pl(lambda e: e.memset(S32[:], 0.0), [], ["S32"])
                    else:
                        DMA(S32[:], st_h[d].rearrange("h k v -> k h v"), [], ["S32"], d_st)
                ab = Abuf[t % 2]
                abk = "Abuf%d" % (t % 2)
                for h in range(4):
                    mm(PS[4][:, 128 * h:128 * h + 128], Kt_[h][:, cols], Qt_[h][:, cols], True, True, [kk[h], qk[h]], ["ps4"])
                Vv(lambda e, ab=ab: e.tensor_tensor(out=ab[:], in0=PS[4][:, 0:512], in1=maskf4[:], op=ALU.mult), ["ps4", "maskf4"], [abk])
                for h in range(4):
                    mm(PS[6][:, 128 * h:128 * h + 128], Kt_[h][:, cols], identb[:], True, True, [kk[h], "identb"], ["ps6"])
                A(lambda e, t=t: e.activation(out=Ktok[:, t, :], in_=PS[6][:, 0:512], func=AF.Copy), ["ps6"], ["Ktok"])
                sins = []
                for c in range(2):
                    gc = 2 * t + c
                    rows = slice(64 * c, 64 * c + 64)
                    for h in range(4):
                        mm(PS[6][:, 128 * h:128 * h + 128], Ktok[rows, t, 128 * h:128 * h + 128], V_[rows, t, 128 * h:128 * h + 128],
                           True, True, ["Ktok", kV], ["ps6"])
                    si = (2 * t + c) % 4
                    sins.append(si)
                    A(lambda e, si=si: e.activation(out=Sin[si][:], in_=S32[:], func=AF.Copy), ["S32"], ["Sin%d" % si])
                    Vv(lambda e: e.tensor_tensor(out=S32[:].rearrange("p h v -> p (h v)"), in0=S32[:].rearrange("p h v -> p (h v)"),
                                                 in1=PS[6][:, 0:512], op=ALU.add), ["S32", "ps6"], ["S32"])
                    Vv(lambda e, gc=gc: e.tensor_tensor(out=S32[:], in0=S32[:], in1=bc(EB_[:, gc, :], [128, 4, 128], 2), op=ALU.mult),
                       ["S32", kEB], ["S32"])
                if is_end(t):
                    DMA(nh[seq_of(t), d].rearrange("h k v -> k h v"), S32[:], ["S32"], [], d_so)
                if sw == 2:
                    for h in range(4):
                        hs = slice(128 * h, 128 * h + 128)
                        mm(PS[7][:, hs], V_[:, t, hs], ab[:, hs], True, False, [kV, abk], ["ps7"])
                        for c in range(2):
                            mm(PS[7][:, 128 * h + 64 * c:128 * h + 64 * c + 64], Sin[sins[c]][:, h, :],
                               Qt_[h][:, 128 * t + 64 * c:128 * t + 64 * c + 64], False, False, ["Sin%d" % sins[c], qk[h]], ["ps7"])
                        mm(PS[7][:, hs], OBT[:, t % 2, hs], Jb[:], False, True, ["OBT%d" % (t % 2), "Jb"], ["ps7"])
                    A(lambda e: e.activation(out=osb[:], in_=PS[7][:, 0:512], func=AF.Copy), ["ps7"], ["te"])
                    A(lambda e: e.activation(out=osq[:], in_=PS[7][:, 0:512], func=AF.Square), ["ps7"], ["osq"])
                    for h in range(4):
                        hs = slice(128 * h, 128 * h + 128)
                        mm(PS[4][:, hs], ones128[:], osq[:, hs], True, True, ["ones128", "osq"], ["ps4"])
                    A(lambda e: e.activation(out=TT[0][:], in_=PS[4][:, 0:512], func=AF.Ln, bias=EPS), ["ps4"], ["ta"])
                    A(lambda e: e.activation(out=TT[0][:], in_=TT[0][:], func=AF.Exp, scale=-0.5), ["ta"], ["ta"])
                    Vv(lambda e: e.tensor_tensor(out=osb[:], in0=osb[:], in1=TT[0][:], op=ALU.mult), ["te", "ta"], ["te"])
                    for h in range(4):
                        Vv(lambda e, h=h, cols=cols: e.scalar_tensor_tensor(out=OA[h][:, cols], in0=osb[:, 128 * h:128 * h + 128], scalar=hg[:, h:h + 1],
                                                                            in1=GA[h][:, cols], op0=ALU.mult, op1=ALU.mult),
                           ["te", "hg", "GA%d" % h], ["OA%d" % h])
                else:
                    for h in range(4):
                        hs = slice(128 * h, 128 * h + 128)
                        mm(PS[7][:, hs], ab[:, hs], V_[:, t, hs], True, False, [abk, kV], ["ps7"])
                        for c in range(2):
                            mm(PS[7][64 * c:64 * c + 64, hs], Qt_[h][:, 128 * t + 64 * c:128 * t + 64 * c + 64], Sin[sins[c]][:, h, :],
                               False, True, ["Sin%d" % sins[c], qk[h]], ["ps7"])
                    ob = OST[0]
                    A(lambda e, ob=ob: e.activation(out=ob[:], in_=PS[7][:, 0:512], func=AF.Copy), ["ps7"], ["OST0"])
                    DMA(obscr[blk * 4 + t], ob[:], ["OST0"], ["obscr%d" % (blk * 4 + t)], d_ob)

            MMb = [MM, MM]
            MMk = ["MM", "MM"]

            Wb = [W, W1[:]]
            Wk = ["W", "W1"]

            def s5_A1(t, c):
                gc = 2 * t + c
                Wt, wkey = Wb[gc % 2], Wk[gc % 2]
                cc_ = slice(128 * t + 64 * c, 128 * t + 64 * c + 64)
                for a in range(4):
                    for part in range(2):
                        oc_ = slice(64 * (2 * a + part), 64 * (2 * a + part) + 64)
                        for q in range(4):
                            if q < 3:
                                mm(PS[q][:, oc_], BTt[a][part][32 * q:32 * q + 32, :],
                                   uT_[a][32 * q:32 * q + 32, cc_], True, True, ["BTt", uk[a]], ["ps%d" % q])
                            else:
                                mm(PS[q][:, oc_], BTt3[a][part][64:128, :],
                                   uT_[a][64:128, cc_], True, True, ["BTt", uk[a]], ["ps%d" % q])
                Wv = Wt.rearrange("p r (a q) j -> p r a q j", q=4)
                for q in range(4):
                    A(lambda e, q=q, Wv=Wv: e.activation(out=Wv[:, :, :, q, :].rearrange("p r a j -> p a r j"),
                                                         in_=PS[q][:, 0:512].rearrange("p (a r j) -> p a r j", a=4, r=2),
                                                         func=AF.Copy), ["ps%d" % q], [wkey])

            def s5_mod(t, c):
                gc = 2 * t + c
                Wt, wkey = Wb[gc % 2], Wk[gc % 2]
                mmv, mk = MMb[gc % 2], MMk[gc % 2]
                wk = [mk] if mk == "MM" else ["XN", "XNb", "XNc"]
                Vv(lambda e, mmv=mmv, Wt=Wt: e.tensor_tensor(out=mmv[:, 0], in0=Wt[:, 0], in1=Tc[:], op=ALU.mult), [wkey, "Tc"], wk)
                Vv(lambda e, mmv=mmv, Wt=Wt: e.tensor_tensor(out=mmv[:, 1], in0=Wt[:, 1], in1=Tc[:], op=ALU.mult), [wkey, "Tc"], wk)
                Vv(lambda e, Wt=Wt: e.tensor_tensor(out=Wt[:, 0], in0=Wt[:, 0], in1=Ts[:], op=ALU.mult), [wkey, "Ts"], [wkey])
                Vv(lambda e, Wt=Wt: e.tensor_tensor(out=Wt[:, 1], in0=Wt[:, 1], in1=Ts[:], op=ALU.mult), [wkey, "Ts"], [wkey])
                Vv(lambda e, mmv=mmv, Wt=Wt: e.tensor_tensor(out=mmv[:, 1], in0=mmv[:, 1], in1=Wt[:, 0], op=ALU.subtract), [wkey] + wk, wk)
                Vv(lambda e, mmv=mmv, Wt=Wt: e.tensor_tensor(out=mmv[:, 0], in0=mmv[:, 0], in1=Wt[:, 1], op=ALU.add), [wkey] + wk, wk)

            def final_state(t):
                if True:
                    seq = seq_of(t)
                    Vv(lambda e: e.tensor_tensor(out=Zo[:, 0, :], in0=Z[:, 0, :], in1=Tc[:, :, 63], op=ALU.mult), ["Z", "Tc"], ["Zo"])
                    Vv(lambda e: e.tensor_tensor(out=zt[1][:, 0, :], in0=Z[:, 1, :], in1=Ts[:, :, 63], op=ALU.mult), ["Z", "Ts"], ["zt1"])
                    Vv(lambda e: e.tensor_tensor(out=Zo[:, 0, :], in0=Zo[:, 0, :], in1=zt[1][:, 0, :], op=ALU.subtract), ["Zo", "zt1"], ["Zo"])
                    Vv(lambda e: e.tensor_tensor(out=Zo[:, 1, :], in0=Z[:, 0, :], in1=Ts[:, :, 63], op=ALU.mult), ["Z", "Ts", "Zo"], ["Zo"])
                    Vv(lambda e: e.tensor_tensor(out=zt[1][:, 0, :], in0=Z[:, 1, :], in1=Tc[:, :, 63], op=ALU.mult), ["Z", "Tc", "Zo"], ["zt1"])
                    Vv(lambda e: e.tensor_tensor(out=Zo[:, 1, :], in0=Zo[:, 1, :], in1=zt[1][:, 0, :], op=ALU.add), ["Zo", "zt1"], ["Zo"])
                    for gl in range(2):
                        DMA(nre[seq, d].rearrange("(P two) p -> two p P", two=2)[gl], Zo[64 * gl:64 * gl + 64, 0, :], ["Zo"], [], d_so, slow=True)
                        DMA(nim[seq, d].rearrange("(P two) p -> two p P", two=2)[gl], Zo[64 * gl:64 * gl + 64, 1, :], ["Zo"], [], d_so, slow=True)

            def s5_chain(t, c):
                gc = 2 * t + c
                mmv, mk = MMb[gc % 2], MMk[gc % 2]
                wk = [mk] if mk == "MM" else ["XN", "XNb", "XNc"]
                if c == 0 and is_start(t):
                    if blk >= 8:
                        Vv(lambda e: e.memset(Z[:], 0.0), [], ["Z"])
                    else:
                        for gl in range(2):
                            DMA(Z[64 * gl:64 * gl + 64, 0, :], st_re[d].rearrange("(P two) p -> two p P", two=2)[gl], [], ["Z"], d_st, slow=True)
                            DMA(Z[64 * gl:64 * gl + 64, 1, :], st_im[d].rearrange("(P two) p -> two p P", two=2)[gl], [], ["Z"], d_st, slow=True)
                        Vv(lambda e: e.tensor_copy(out=Z[:, 2, :], in_=Z[:, 0, :]), ["Z"], ["Z"])
                if eblvl < 5.3:
                    return
                first = (c == 0 and is_start(t))
                L1, L2 = (LL1, LL2) if first else (LG1, LG2)
                lk = ["LL1", "LL2"] if first else ["LG1", "LG2"]
                Vv(lambda e, L1=L1: e.tensor_tensor(out=zt[0][:], in0=L1[:], in1=Z[:, 0:2, :], op=ALU.mult), [lk[0], "Z"], ["zt0"])
                Vv(lambda e, L2=L2: e.tensor_tensor(out=zt[1][:], in0=L2[:], in1=Z[:, 1:3, :], op=ALU.mult), [lk[1], "Z"], ["zt1"])
                Vv(lambda e: e.tensor_tensor(out=zt[0][:], in0=zt[0][:], in1=zt[1][:], op=ALU.add), ["zt0", "zt1"], ["zt0"])
                Vv(lambda e, mmv=mmv: e.tensor_tensor(out=mmv[:, :, :, 0], in0=mmv[:, :, :, 0], in1=zt[0][:], op=ALU.add), wk + ["zt0"], wk)
                for part in range(2):
                    Vv(lambda e, part=part, mmv=mmv: e.tensor_tensor_scan(
                        out=XP[:, part].rearrange("p a j -> p (a j)"), data0=Mg[:].rearrange("p a j -> p (a j)"),
                        data1=mmv[:, part].rearrange("p a j -> p (a j)"), initial=0.0, op0=ALU.mult, op1=ALU.add),
                       wk + ["Mg"], ["XP"])
                Vv(lambda e: e.tensor_copy(out=Z[:, 0:2, :], in_=XP[:, :, :, 63]), ["XP"], ["Z"])
                Vv(lambda e: e.tensor_copy(out=Z[:, 2, :], in_=XP[:, 0, :, 63]), ["XP", "Z"], ["Z"])
                A(lambda e: e.activation(out=XPb[:], in_=XP[:], func=AF.Copy), ["XP"], ["XPb"])
                if c == 1 and is_end(t):
                    final_state(t)

            def load_ybt(t):
                if sw == 2:
                    DMA(YBT[:, t % 2, :], ybscr[blk * 4 + (3 - t)], ["ybscr%d" % (blk * 4 + 3 - t)], ["YBT%d" % (t % 2)], d_ybl)

            def s5_demod(t, c):
                Vv(lambda e: e.tensor_tensor(out=XH[:, 0], in0=XPb[:, 0], in1=Tcb[:], op=ALU.mult), ["XPb", "Tcb"], ["XH"])
                Vv(lambda e: e.tensor_tensor(out=ftmp[:], in0=XPb[:, 1], in1=Tsb[:], op=ALU.mult), ["XPb", "Tsb"], ["ftmp"])
                Vv(lambda e: e.tensor_tensor(out=XPb[:, 0], in0=XPb[:, 0], in1=Tsb[:], op=ALU.mult), ["XPb", "Tsb"], ["XPb"])
                Vv(lambda e: e.tensor_tensor(out=XPb[:, 1], in0=XPb[:, 1], in1=Tcb[:], op=ALU.mult), ["XPb", "Tcb"], ["XPb"])
                Vv(lambda e: e.tensor_tensor(out=XH[:, 0], in0=XH[:, 0], in1=ftmp[:], op=ALU.subtract), ["XH", "ftmp"], ["XH"])
                Vv(lambda e: e.tensor_tensor(out=XH[:, 1], in0=XPb[:, 0], in1=XPb[:, 1], op=ALU.add), ["XPb", "XH"], ["XH"])
            def s5_y(t, c):
                cc_ = slice(128 * t + 64 * c, 128 * t + 64 * c + 64)
                if sw == 2:
                    for a in range(4):
                        oc = slice(128 * a + 64 * c, 128 * a + 64 * c + 64)
                        mm(PS[5][:, oc], Ddiag[a][:], uT_[a][:, cc_], True, False, ["Ddiag%d" % a, uk[a]], ["ps5"])
                        for q in range(4):
                            for part in range(2):
                                if q < 3:
                                    mm(PS[5][32 * q:32 * q + 32, oc], Cst[a][part][:, 32 * q:32 * q + 32], XH[:, part, 4 * a + q, :],
                                       False, False, ["Cst", "XH"], ["ps5"])
                                else:
                                    mm(PS[5][64:128, oc], Cst3[a][part][:, 0:64], XH[:, part, 4 * a + q, :],
                                       False, False, ["Cst", "XH"], ["ps5"])
                        mm(PS[5][:, oc], YBT[:, t % 2, 128 * a:128 * a + 128], Jb[:, 64 * c:64 * c + 64], False, True, ["YBT%d" % (t % 2), "Jb"], ["ps5"])
                else:
                    for a in range(4):
                        for q in range(4):
                            for part in range(2):
                                mm(PS[5][64 * c:64 * c + 64, 128 * a + 32 * q:128 * a + 32 * q + 32], XH[:, part, 4 * a + q, :],
                                   Cst[a][part][:, 32 * q:32 * q + 32], part == 0, part == 1, ["Cst", "XH"], ["ps5"])

            def s5_post(t):
                cols = slice(128 * t, 128 * t + 128)
                if sw == 1:
                    yb = OST[1]
                    A(lambda e, yb=yb: e.activation(out=yb[:], in_=PS[5][:, 0:512], func=AF.Copy), ["ps5"], ["OST1"])
                    DMA(ybscr[blk * 4 + t], yb[:], ["OST1"], ["ybscr%d" % (blk * 4 + t)], d_yb)
                else:
                    t1 = TT[1]
                    A(lambda e: e.activation(out=t1[:], in_=PS[5][:, 0:512], func=AF.Square), ["ps5"], ["tb"])
                    Vv(lambda e: e.tensor_scalar(out=t1[:], in0=t1[:], scalar1=0.044715, scalar2=1.0, op0=ALU.mult, op1=ALU.add), ["tb"], ["tb"])
                    Vv(lambda e: e.tensor_tensor(out=t1[:], in0=t1[:], in1=PS[5][:, 0:512], op=ALU.mult), ["tb", "ps5"], ["tb"])
                    A(lambda e: e.activation(out=t1[:], in_=t1[:], func=AF.Sigmoid, scale=1.5957691216057308), ["tb"], ["tb"])
                    Vv(lambda e: e.tensor_tensor(out=Y2b[:], in0=t1[:], in1=PS[5][:, 0:512], op=ALU.mult), ["tb", "ps5"], ["Y2b"])
                    for ao in range(4):
                        for ai in range(4):
                            mm(PS[6][:, 128 * ao:128 * ao + 128], WG[:, ai, 128 * ao:128 * ao + 128], Y2b[:, 128 * ai:128 * ai + 128],
                               ai == 0, ai == 3, ["WG", "Y2b"], ["ps6"])
                    for ao in range(4):
                        A(lambda e, ao=ao: e.activation(out=sg2[:, 128 * ao:128 * ao + 128], in_=PS[6][:, 128 * ao:128 * ao + 128],
                                                        func=AF.Sigmoid, bias=bgluT[:, ao:ao + 1]), ["ps6", "bgluT"], ["osq"])
                    Vv(lambda e: e.tensor_tensor(out=sg2[:], in0=sg2[:], in1=Y2b[:], op=ALU.mult), ["osq", "Y2b"], ["osq"])
                    for a in range(4):
                        Vv(lambda e, a=a, cols=cols: e.tensor_tensor(out=OB[a][:, cols], in0=sg2[:, 128 * a:128 * a + 128], in1=GBs[a][:, cols], op=ALU.mult),
                           ["osq", "GBs%d" % a], ["OB%d" % a])

            pr = {"g": prep_rest, "pending": []}

            def prep_done():
                return pr["g"] is None

            def pull_prep(n):
                for _ in range(n):
                    if pr["g"] is None:
                        return
                    try:
                        next(pr["g"])
                    except StopIteration:
                        pr["g"] = None

            def finish_prep():
                while pr["g"] is not None:
                    pull_prep(1)

            def want_hgrn(t):
                pr["pending"].append(t)

            def flush_hgrn(maxn):
                n = 0
                while pr["pending"] and prep_done() and n < maxn:
                    hgrn_tile(pr["pending"].pop(0))
                    n += 1

            want_hgrn(0)
            flush_hgrn(1)
            if eblvl >= 5:
                load_ybt(0)
                load_ybt(1)
                s5_A1(0, 0)
                s5_A1(0, 1)
                s5_mod(0, 0)
            for gc in range(8):
                t, c = divmod(gc, 2)
                if gc + 2 < 8:
                    t2, c2 = divmod(gc + 2, 2)
                    if c2 == 0:
                        want_hgrn(t2)
                        flush_hgrn(1)
                    if eblvl >= 5:
                        s5_A1(t2, c2)
                if eblvl >= 5:
                    s5_chain(t, c)
                    if gc + 1 < 8:
                        s5_mod(*divmod(gc + 1, 2))
                    if gc >= 1:
                        tp_, cp_ = divmod(gc - 1, 2)
                        s5_y(tp_, cp_)
                        if cp_ == 1:
                            finish_prep()
                            s5_post(tp_)
                            if tp_ + 2 < 4:
                                load_ybt(tp_ + 2)
                    s5_demod(t, c)
                    if gc == 7:
                        s5_y(t, c)
                        finish_prep()
                        s5_post(t)
                pull_prep(ppull)
                flush_hgrn(2)
                if gen is not None and prep_done():
                    for _ in range(npull):
                        tok = next(gen, "DONE")
                        if tok in ("XNFREE", "DONE") and on_token is not None:
                            on_token(tok)
            finish_prep()
            flush_hgrn(8)

        def even_block(blk, d, sw, bs):
            for _ in even_prep(blk, d, sw, bs):
                pass
            even_tiles(blk, d, sw, bs)

        def run_until_ready(g):
            for tok in g:
                if tok == "READY":
                    return

        def out_proj_gen(ubase, srcs, srckeys):
            def mms(n):
                slot = load_unit(ubase + n)
                bank = n % 4
                for t in range(4):
                    for k in range(8):
                        sk = srckeys[k] if isinstance(srckeys[k], list) else [srckeys[k]]
                        mm(PS[bank][:, 128 * t:128 * t + 128], srcs[k][:, 128 * t:128 * t + 128], WR[slot][:, k, :], k == 0, k == 7,
                           ["wr%d" % slot] + sk, ["ps%d" % bank])

            def add(n):
                bank = n % 4
                Vv(lambda e, n=n, bank=bank: e.tensor_tensor(out=X[:, :, 128 * n:128 * n + 128],
                                                             in0=PS[bank][:, 0:512].rearrange("p (t c) -> p t c", c=128),
                                                             in1=X[:, :, 128 * n:128 * n + 128], op=ALU.add), ["ps%d" % bank, "X"], ["X"])

            for n in range(0, 8, 2):
                mms(n)
                mms(n + 1)
                add(n)
                add(n + 1)
                yield

        def out_proj(ubase, srcs, srckeys):
            for _ in out_proj_gen(ubase, srcs, srckeys):
                pass

        XNv = XN[:].rearrange("p t d -> p (t d)").rearrange("p (f c) -> p f c", c=512)

        def reload_x(blk):
            DMA(X[:], xs[512 * blk:512 * blk + 512, :].rearrange("(t p) d -> p t d", p=128), [], ["X"] + XTk, d_x)

        def odd_gen(blk):
            vec = 0 if blk < 8 else 1
            yield from front_gen(blk, 1, vec, False, False)
            L = 64 if blk < 8 else 256
            cv, r_ = TT[2], TT[3]
            stages = []
            for f in range(8):
                stages += [(U_ODIN + 4 * f + 1, "cg", f), (U_ODIN + 4 * f + 2, "v", f), (U_ODIN + 4 * f + 3, "g", f), (U_ODIN + 4 * f + 0, "bg", f)]
            SK = 2

            def consume(s):
                u, kind, f = stages[s]
                b = s % 4
                y1 = XNv[:, f, :]
                if kind == "cg":
                    A(lambda e, b=b: e.activation(out=cv[:], in_=PS[b][:, 0:512], func=AF.Copy), ["ps%d" % b], ["tc"])
                elif kind == "v":
                    Vv(lambda e, b=b: e.tensor_tensor(out=cv[:], in0=cv[:], in1=PS[b][:, 0:512], op=ALU.mult), ["tc", "ps%d" % b], ["tc"])
                    Vv(lambda e, f=f: e.tensor_scalar(out=r_[:], in0=cv[:], scalar1=cwT[:, 1, f:f + 1], scalar2=cbT[:, f:f + 1],
                                                      op0=ALU.mult, op1=ALU.add), ["tc", "cwT", "cbT"], ["td"])
                    cv3 = cv[:].rearrange("p (r j) -> p r j", j=L)
                    r3 = r_[:].rearrange("p (r j) -> p r j", j=L)
                    Vv(lambda e, f=f, cv3=cv3, r3=r3: e.scalar_tensor_tensor(out=r3[:, :, 1:L], in0=cv3[:, :, 0:L - 1], scalar=cwT[:, 0, f:f + 1],
                                                                            in1=r3[:, :, 1:L], op0=ALU.mult, op1=ALU.add),
                       ["tc", "td", "cwT"], ["td"])
                    Vv(lambda e, f=f, cv3=cv3, r3=r3: e.scalar_tensor_tensor(out=r3[:, :, 0:L - 1], in0=cv3[:, :, 1:L], scalar=cwT[:, 2, f:f + 1],
                                                                            in1=r3[:, :, 0:L - 1], op0=ALU.mult, op1=ALU.add),
                       ["tc", "td", "cwT"], ["td"])
                elif kind == "g":
                    A(lambda e, b=b, y1=y1: e.activation(out=y1, in_=PS[b][:, 0:512], func=AF.Silu), ["ps%d" % b], XK)
                else:
                    Vv(lambda e, b=b: e.tensor_tensor(out=r_[:], in0=r_[:], in1=PS[b][:, 0:512], op=ALU.mult), ["td", "ps%d" % b], ["td"])
                    Vv(lambda e, y1=y1: e.tensor_tensor(out=y1, in0=r_[:], in1=y1, op=ALU.mult), ["td"] + XK, XK)

            for f in range(8):
                base = 4 * f
                proj_fm(stages[base + 0][0], 0)
                proj_fm(stages[base + 1][0], 1)
                proj_fm(stages[base + 2][0], 2)
                consume(base + 0)
                proj_fm(stages[base + 3][0], 3)
                consume(base + 1)
                consume(base + 2)
                consume(base + 3)
                yield
            for _ in out_proj_gen(U_ODOUT + 8 * vec, [XNv[:, f, :] for f in range(8)], [XK] * 8):
                yield
            yield "XNFREE"
            for t in range(4):
                for hf in range(2):
                    A(lambda e, t=t, hf=hf: e.activation(out=OST[hf][:], in_=X[:, t, 512 * hf:512 * hf + 512], func=AF.Square,
                                                         accum_out=ss8[:, 2 * t + hf:2 * t + hf + 1]),
                      ["X"], ["OST%d" % hf, "ss8"])
            Vv(lambda e: e.tensor_tensor(out=ss[:], in0=ss8[:].rearrange("p (t h) -> p t h", h=2)[:, :, 0],
                                         in1=ss8[:].rearrange("p (t h) -> p t h", h=2)[:, :, 1], op=ALU.add), ["ss8"], ["ss"])
            A(lambda e: e.activation(out=rstd[:], in_=ss[:], func=AF.Ln, scale=1.0 / 1024, bias=EPS), ["ss"], ["rstd"])
            A(lambda e: e.activation(out=rstd[:], in_=rstd[:], func=AF.Exp, scale=-0.5), ["rstd"], ["rstd"])
            yield
            for t in range(4):
                Vv(lambda e, t=t: e.scalar_tensor_tensor(out=X[:, t, :], in0=X[:, t, :], scalar=rstd[:, t:t + 1], in1=FG[:],
                                                         op0=ALU.mult, op1=ALU.mult), ["X", "rstd", "FG"], ["X"])
                yield
            DMA(ys[512 * blk:512 * blk + 512, :].rearrange("(t p) d -> p t d", p=128), X[:], ["X"], [], d_out)

        XT = [X[:, i // 2, 512 * (i % 2):512 * (i % 2) + 512] for i in range(5)]
        XTk = ["XB%d" % i for i in range(5)]
        BS0 = dict(V=V, uT=uT, Qt=Qt, Kt=Kt, EB=EB, kV="V", uk=["uT%d" % a for a in range(4)], qk=["Qt%d" % h for h in range(4)],
                   kk=["Kt%d" % h for h in range(4)], kEB="EB", T2=TT, T2k=["ta", "tb", "tc", "td", "te"])
        BS1 = dict(V=GAall, uT=GBs, Qt=OA, Kt=OB, EB=EB2, kV="GAall", uk=["GBs%d" % a for a in range(4)], qk=["OA%d" % h for h in range(4)],
                   kk=["OB%d" % h for h in range(4)], kEB="EB2", T2=XT, T2k=XTk)
        BS0s1 = dict(BS0, T2=XT, T2k=XTk)
        for q in range(4):
            Vv(lambda e, q=q: e.memset(M4[q][:], 0.0), [], ["M4_%d" % q])
        for ci in range(2):
            Vv(lambda e, ci=ci: e.memset(Cpads[ci][:], 0.0), [], ["Cpad%d_%d" % (ci, g8) for g8 in range(8)])
        for a in range(4):
            for p in range(2):
                Vv(lambda e, a=a, p=p: e.memset(Cst3[a][p][:], 0.0), [], ["Cst"])

        if stage >= 1:
            s5_consts(1)
        cast_all()
        blks1 = list(range(NBLK - 1, NBLK - 1 - (nblk1 if stage >= 2 else 0), -1))
        if blks1:
            sets = [BS0s1, BS1]
            A(lambda e: e.activation(out=X[0:1, 0, 0:2], in_=CF[0:1, 0:2], func=AF.Copy), ["CF"], ["X"] + XTk)
            for _ in even_prep(blks1[0], 1, 1, sets[0]):
                pass
            for i, blk in enumerate(blks1):
                gen = even_prep(blks1[i + 1], 1, 1, sets[(i + 1) % 2]) if i + 1 < len(blks1) else None
                even_tiles(blk, 1, 1, sets[i % 2], gen, npull=3)
                if gen is not None:
                    for _ in gen:
                        pass
        if stage >= 3:
            s5_consts(0)
        og = None
        nb2 = nblk2 if stage >= 4 else 0
        pg = None
        if nb2 > 0:
            pg = even_prep(0, 0, 2, BS0)
            run_until_ready(pg)
        for blk in range(nb2):
            st = {"front": False, "reload": False, "pg": None}

            def do_front(blk=blk, st=st):
                if not st["front"]:
                    st["front"] = True
                    if blk + 1 < nb2:
                        st["pg"] = even_prep(blk + 1, 0, 2, BS0)
                        for tok_ in st["pg"]:
                            if tok_ == "FRONT":
                                break

            def do_reload(blk=blk, st=st):
                if not st["reload"]:
                    st["reload"] = True
                    reload_x(blk)

            def on_token(tok, do_front=do_front, do_reload=do_reload):
                if tok == "XNFREE":
                    do_front()
                elif tok == "DONE":
                    do_reload()

            even_tiles(blk, 0, 2, BS0, og, npull=4, prep_rest=pg, ppull=7, on_token=on_token)
            if og is not None:
                for tok in og:
                    if tok == "XNFREE":
                        do_front()
            vec = 0 if blk < 8 else 1
            do_reload()
            do_front()
            pg = st["pg"]
            if dbg and blk == 8:
                for i_ in range(4):
                    DMA(dbg_bf[i_], OA[i_], ["OA%d" % i_], [], d_so)
                    DMA(dbg_bf[4 + i_], OB[i_], ["OB%d" % i_], [], d_so)
            out_proj(U_EVOUT + 8 * vec, [OA[h] for h in range(4)] + [OB[a] for a in range(4)],
                     ["OA%d" % h for h in range(4)] + ["OB%d" % a for a in range(4)])
            if dbg and blk == 8:
                DMA(dbg_x, X[:], ["X"], [], d_so)
            if pg is not None:
                run_until_ready(pg)
            og = odd_gen(blk)
        if og is not None:
            for _ in og:
                pass

        S.emit()
    return nc


_NC_CACHE = {}


def kernel(x_prompt, x_sample, state_hgrn, state_s5_re, state_s5_im, c, c_ctx, norm_g, w_mod, b_mod,
           w_in_even, w_out_even, lb_logits, hgrn_norm_g, s5_lam_re, s5_lam_im, s5_log_dt,
           s5_b_re, s5_b_im, s5_c_re, s5_c_im, s5_d, w_glu, b_glu, w_in_odd, w_out_odd,
           conv_w, conv_b, final_norm_g):
    f = lambda a: np.ascontiguousarray(np.asarray(a, dtype=np.float32))
    if "nc" not in _NC_CACHE:
        _NC_CACHE["nc"] = build_nc()
    nc = _NC_CACHE["nc"]
    cst = _consts()
    shared = {
        "norm_g": f(norm_g), "w_mod": f(w_mod), "b_mod": f(b_mod), "w_in_even": f(w_in_even[0]), "w_out_even": f(w_out_even[0]),
        "lb_logits": f(lb_logits), "hgrn_norm_g": f(hgrn_norm_g[0]), "s5_lam_re": f(s5_lam_re[0]), "s5_lam_im": f(s5_lam_im[0]),
        "s5_log_dt": f(s5_log_dt[0]), "s5_b_re": f(s5_b_re[0]), "s5_b_im": f(s5_b_im[0]), "s5_c_re": f(s5_c_re[0]),
        "s5_c_im": f(s5_c_im[0]), "s5_d": f(s5_d[0]), "w_glu": f(w_glu[0]), "b_glu": f(b_glu[0]), "w_in_odd": f(w_in_odd[0]),
        "w_out_odd": f(w_out_odd[0]), "conv_w": f(conv_w[0]), "conv_b": f(conv_b[0]), "final_norm_g": f(final_norm_g),
        "consts": cst,
    }
    xp = f(x_prompt)
    xsm = f(x_sample)
    in_maps = []
    for i in range(8):
        m = dict(shared)
        m["xs"] = np.concatenate([xsm[i], xp[4 * i:4 * i + 4].reshape(1024, 1024)], axis=0)
        m["cvec"] = np.stack([f(c)[i], f(c_ctx)], axis=0)
        m["st_h"] = f(state_hgrn)[i, 0]
        m["st_re"] = f(state_s5_re)[i, 0]
        m["st_im"] = f(state_s5_im)[i, 0]
        in_maps.append(m)
    res = run_bass_kernel_spmd(nc, in_maps, core_ids=list(range(8)))
    y_prompt = np.zeros((32, 256, 1024), np.float32)
    y_sample = np.zeros((8, 4096, 1024), np.float32)
    nh = np.zeros((32, 1, 2, 4, 128, 128), np.float32)
    nre = np.zeros((32, 1, 2, 32, 64), np.float32)
    nim = np.zeros((32, 1, 2, 32, 64), np.float32)
    for i in range(8):
        r = res.results[i]
        y_sample[i] = r["ys"][:4096]
        y_prompt[4 * i:4 * i + 4] = r["ys"][4096:].reshape(4, 256, 1024)
        nh[4 * i:4 * i + 4, 0] = r["nh"]
        nre[4 * i:4 * i + 4, 0] = r["nre"]
        nim[4 * i:4 * i + 4, 0] = r["nim"]
    return (y_prompt, y_sample, nh, nre, nim)
```
